# Optimizing a Trainium2 kernel written in Bass

```python
import jax, jax.numpy as jnp
from jax import lax
import numpy as np

D_MODEL = 2048
BATCH = 32
SEQ = 256
DEPTH = 4
DEC_BATCH = 4
DEC_SEQ = 4096
PAST_LEN = 256

GRID_W = 64
N_MIXERS = 2
EXPAND = 2
D_INNER = EXPAND * D_MODEL
N_LRU = (DEPTH + 1) // 2
N_RWKV = DEPTH // 2
LRU_BLOCKS = 16
LRU_BLOCK = D_INNER // LRU_BLOCKS
CONV_W = 4
CONV_LEFT = 2
LRU_C = 8.0
RWKV_HEAD = 64
RWKV_HEADS = D_INNER // RWKV_HEAD
LORA_DECAY = 128
LORA_A = 128
LORA_V = 96
NORM_EPS = 1e-6
GN_EPS = 64e-5

kernel_name = 'hybrid_rglru_rwkv7_diffusion_step'


def rmsnorm(x, g):
    xf = x.astype(jnp.float32)
    y = xf * lax.rsqrt(jnp.mean(xf * xf, axis=-1, keepdims=True) + NORM_EPS)
    return (y * g.astype(jnp.float32)).astype(x.dtype)


def adaln(cond, w, b):
    m = (jax.nn.silu(cond) @ w + b)[:, None, :]
    return jnp.split(m, 3, axis=-1)


def shift_sequence(x):
    half = x.shape[-1] // 2
    prev = jnp.pad(x[:, :-1, :half], ((0, 0), (1, 0), (0, 0)))
    nxt = jnp.pad(x[:, 1:, half:], ((0, 0), (0, 1), (0, 0)))
    return jnp.concatenate([prev, nxt], axis=-1)


def shift_grid(x):
    bsz, t, d = x.shape
    rows = t // GRID_W
    q = d // 4
    g = x.reshape(bsz, rows, GRID_W, d)
    left = jnp.pad(g[:, :, :-1, :q], ((0, 0), (0, 0), (1, 0), (0, 0)))
    right = jnp.pad(g[:, :, 1:, q:2 * q], ((0, 0), (0, 0), (0, 1), (0, 0)))
    up = jnp.pad(g[:, :-1, :, 2 * q:3 * q], ((0, 0), (1, 0), (0, 0), (0, 0)))
    down = jnp.pad(g[:, 1:, :, 3 * q:], ((0, 0), (0, 1), (0, 0), (0, 0)))
    return jnp.concatenate([left, right, up, down], axis=-1).reshape(bsz, t, d)


def conv_centred(x, w, b):
    t = x.shape[1]
    xp = jnp.pad(x, ((0, 0), (CONV_LEFT, CONV_W - 1 - CONV_LEFT), (0, 0)))
    y = b
    for tap in range(CONV_W):
        y = y + xp[:, tap:tap + t] * w[tap]
    return y


def lru_scan(a, u, h0, reverse):
    def step(h, inp):
        at, ut = inp
        h = at * h + ut
        return h, h
    h_final, hs = lax.scan(step, h0, (jnp.swapaxes(a, 0, 1), jnp.swapaxes(u, 0, 1)), reverse=reverse)
    return jnp.swapaxes(hs, 0, 1), h_final


def lru_mixer(h, h0, P, j):
    bsz, t, _ = h.shape
    x, z = jnp.split(h @ P['lru_w_in'][j], 2, axis=-1)
    x = conv_centred(x, P['lru_conv_w'][j], P['lru_conv_b'][j])
    xb = x.reshape(bsz, t, LRU_BLOCKS, LRU_BLOCK)
    xf = x.astype(jnp.float32)
    ys, finals = [], []
    for d in range(2):
        gw, gb = P['lru_gate_w'][j, d], P['lru_gate_b'][j, d]
        r = jax.nn.sigmoid(jnp.einsum('btnc,nce->btne', xb, gw[0]).reshape(bsz, t, D_INNER) + gb[0])
        i = jax.nn.sigmoid(jnp.einsum('btnc,nce->btne', xb, gw[1]).reshape(bsz, t, D_INNER) + gb[1])
        log_a = -LRU_C * r.astype(jnp.float32) * jax.nn.softplus(-P['lru_lambda'][j, d].astype(jnp.float32))
        u = jnp.sqrt(-jnp.expm1(2.0 * log_a)) * (i.astype(jnp.float32) * xf)
        y_d, h_d = lru_scan(jnp.exp(log_a), u, h0[d].astype(jnp.float32), reverse=(d == 1))
        ys.append(y_d)
        finals.append(h_d)
    y = (ys[0] + ys[1]).astype(h.dtype)
    out = (y * jax.nn.silu(z)) @ P['lru_w_out'][j]
    return out, jnp.stack(finals, 0)


def to_heads(t):
    return t.reshape(t.shape[:-1] + (RWKV_HEADS, RWKV_HEAD)).astype(jnp.float32)


def rwkv_scan(r, w, k, v, a, b, s0, reverse):
    def step(s, inp):
        rt, wt, kt, vt, at, bt = inp
        sa = jnp.einsum('bhvk,bhk->bhv', s, at)
        s = s * wt[:, :, None, :] + sa[..., None] * bt[:, :, None, :] + vt[..., None] * kt[:, :, None, :]
        return s, jnp.einsum('bhvk,bhk->bhv', s, rt)
    xs = tuple(jnp.swapaxes(q, 0, 1) for q in (r, w, k, v, a, b))
    s_final, ys = lax.scan(step, s0, xs, reverse=reverse)
    return jnp.swapaxes(ys, 0, 1), s_final


def rwkv_mixer(h, s0, shift_fn, v_first, P, j):
    bsz, t, _ = h.shape
    mu = P['rwkv_mu'][j]
    xx = shift_fn(h) - h
    xr, xw, xk, xv, xa = [h + xx * mu[m] for m in range(5)]
    r = xr @ P['rwkv_w_r'][j]
    k = xk @ P['rwkv_w_k'][j]
    v = xv @ P['rwkv_w_v'][j]
    z = h @ P['rwkv_w_g'][j]
    if j > 0:
        v_mix = jax.nn.sigmoid(P['rwkv_v0'][j - 1] + (xv @ P['rwkv_v1'][j - 1]) @ P['rwkv_v2'][j - 1])
        v = v + (v_first - v) * v_mix
    rh, kh, vh = to_heads(r), to_heads(k), to_heads(v)
    kk = kh * to_heads(P['rwkv_k_k'][j])
    kk = kk / jnp.maximum(jnp.sqrt(jnp.sum(kk * kk, axis=-1, keepdims=True)), 1e-12)
    k_a = to_heads(P['rwkv_k_a'][j])
    r_k = to_heads(P['rwkv_r_k'][j])
    ys, bonus, finals = [], [], []
    for d in range(2):
        w_pre = to_heads(P['rwkv_w0'][j, d] + jnp.tanh(xw @ P['rwkv_w1'][j, d]) @ P['rwkv_w2'][j, d])
        w_log = -jax.nn.softplus(-w_pre) - 0.5
        decay = jnp.exp(-jnp.exp(w_log))
        a = jax.nn.sigmoid(to_heads(P['rwkv_a0'][j, d] + (xa @ P['rwkv_a1'][j, d]) @ P['rwkv_a2'][j, d]))
        kd = kh * (1.0 + (a - 1.0) * k_a)
        y_d, s_d = rwkv_scan(rh, decay, kd, vh, -kk, kk * a, s0[d].astype(jnp.float32), reverse=(d == 1))
        ys.append(y_d)
        bonus.append(jnp.sum(rh * kd * r_k, axis=-1, keepdims=True) * vh)
        finals.append(s_d)
    y = ys[0] + ys[1]
    mean = jnp.mean(y, axis=-1, keepdims=True)
    var = jnp.mean(jnp.square(y - mean), axis=-1, keepdims=True)
    y = (y - mean) * lax.rsqrt(var + GN_EPS) * to_heads(P['rwkv_ln_w'][j]) + to_heads(P['rwkv_ln_b'][j])
    y = (y + bonus[0] + bonus[1]).reshape(bsz, t, D_INNER).astype(h.dtype)
    out = (y * jax.nn.silu(z)) @ P['rwkv_w_o'][j]
    return out, jnp.stack(finals, 0), v


def trunk(x, cond, init_lru, init_rwkv, shift_fn, P):
    fin_lru, fin_rwkv = [], []
    v_first = None
    for i in range(DEPTH):
        shift, scale, gate = adaln(cond, P['ada_w'][i], P['ada_b'][i])
        h = rmsnorm(x, P['norm_pre'][i]) * (1 + scale) + shift
        j = i // N_MIXERS
        if i % N_MIXERS == 0:
            m, fin = lru_mixer(h, init_lru[j], P, j)
            fin_lru.append(fin)
        else:
            m, fin, v = rwkv_mixer(h, init_rwkv[j], shift_fn, v_first, P, j)
            if j == 0:
                v_first = v
            fin_rwkv.append(fin)
        x = x + gate * rmsnorm(m, P['norm_post'][i])
    return x, jnp.stack(fin_lru, 0), jnp.stack(fin_rwkv, 0)


def setup_inputs(seed: int = 0) -> dict:
    key = jax.random.key(seed)
    keys = jax.random.split(key, 48)
    ks = iter([keys[n] for n in range(48)])
    f32 = jnp.float32
    E, H, N, D = D_INNER, RWKV_HEADS, RWKV_HEAD, D_MODEL

    def nrm(shape, s):
        return jax.random.normal(next(ks), shape, f32) * s

    def uni(shape, lo, hi):
        return jax.random.uniform(next(ks), shape, f32, lo, hi)

    inp = {}
    inp['x_prompt'] = nrm((BATCH, SEQ, D), 1.0)
    inp['x_sample'] = nrm((DEC_BATCH, DEC_SEQ, D), 1.0)
    inp['state_lru'] = nrm((DEC_BATCH, N_LRU, 2, E), 0.5)
    inp['state_rwkv'] = nrm((DEC_BATCH, N_RWKV, 2, H, N, N), 0.3)
    inp['c'] = nrm((DEC_BATCH, D), 1.0)
    inp['c_ctx'] = nrm((D,), 1.0)
    inp['ada_w'] = nrm((DEPTH, D, 3 * D), D ** -0.5)
    inp['ada_b'] = nrm((DEPTH, 3 * D), 0.02)
    inp['norm_pre'] = 1.0 + nrm((DEPTH, D), 0.05)
    inp['norm_post'] = 1.0 + nrm((DEPTH, D), 0.05)
    inp['lru_w_in'] = nrm((N_LRU, D, 2 * E), D ** -0.5)
    inp['lru_conv_w'] = nrm((N_LRU, CONV_W, E), CONV_W ** -0.5)
    inp['lru_conv_b'] = nrm((N_LRU, E), 0.02)
    inp['lru_gate_w'] = nrm((N_LRU, 2, 2, LRU_BLOCKS, LRU_BLOCK, LRU_BLOCK), LRU_BLOCK ** -0.5)
    inp['lru_gate_b'] = nrm((N_LRU, 2, 2, E), 0.02)
    a_c = uni((N_LRU, 2, E), 0.9, 0.999)
    a_base = a_c ** (1.0 / LRU_C)
    inp['lru_lambda'] = jnp.log(a_base) - jnp.log1p(-a_base)
    inp['lru_w_out'] = nrm((N_LRU, E, D), E ** -0.5)
    inp['rwkv_mu'] = uni((N_RWKV, 5, D), 0.0, 1.0)
    inp['rwkv_w_r'] = nrm((N_RWKV, D, E), D ** -0.5)
    inp['rwkv_w_k'] = nrm((N_RWKV, D, E), D ** -0.5)
    inp['rwkv_w_v'] = nrm((N_RWKV, D, E), D ** -0.5)
    inp['rwkv_w_g'] = nrm((N_RWKV, D, E), D ** -0.5)
    inp['rwkv_w_o'] = nrm((N_RWKV, E, D), E ** -0.5)
    inp['rwkv_w0'] = uni((N_RWKV, 2, E), -5.0, 1.0)
    inp['rwkv_w1'] = nrm((N_RWKV, 2, D, LORA_DECAY), D ** -0.5)
    inp['rwkv_w2'] = nrm((N_RWKV, 2, LORA_DECAY, E), 0.5 * LORA_DECAY ** -0.5)
    inp['rwkv_a0'] = nrm((N_RWKV, 2, E), 0.1)
    inp['rwkv_a1'] = nrm((N_RWKV, 2, D, LORA_A), D ** -0.5)
    inp['rwkv_a2'] = nrm((N_RWKV, 2, LORA_A, E), 0.5 * LORA_A ** -0.5)
    inp['rwkv_k_k'] = 0.85 + nrm((N_RWKV, E), 0.05)
    inp['rwkv_k_a'] = 1.0 + nrm((N_RWKV, E), 0.05)
    inp['rwkv_r_k'] = nrm((N_RWKV, E), 0.1)
    inp['rwkv_ln_w'] = 1.0 + nrm((N_RWKV, E), 0.05)
    inp['rwkv_ln_b'] = nrm((N_RWKV, E), 0.02)
    inp['rwkv_v0'] = nrm((N_RWKV - 1, E), 0.1)
    inp['rwkv_v1'] = nrm((N_RWKV - 1, D, LORA_V), D ** -0.5)
    inp['rwkv_v2'] = nrm((N_RWKV - 1, LORA_V, E), 0.5 * LORA_V ** -0.5)
    return inp


def reference(x_prompt, x_sample, state_lru, state_rwkv, c, c_ctx,
              ada_w, ada_b, norm_pre, norm_post,
              lru_w_in, lru_conv_w, lru_conv_b, lru_gate_w, lru_gate_b, lru_lambda, lru_w_out,
              rwkv_mu, rwkv_w_r, rwkv_w_k, rwkv_w_v, rwkv_w_g, rwkv_w_o,
              rwkv_w0, rwkv_w1, rwkv_w2, rwkv_a0, rwkv_a1, rwkv_a2,
              rwkv_k_k, rwkv_k_a, rwkv_r_k, rwkv_ln_w, rwkv_ln_b,
              rwkv_v0, rwkv_v1, rwkv_v2):
    P = dict(ada_w=ada_w, ada_b=ada_b, norm_pre=norm_pre, norm_post=norm_post,
             lru_w_in=lru_w_in, lru_conv_w=lru_conv_w, lru_conv_b=lru_conv_b,
             lru_gate_w=lru_gate_w, lru_gate_b=lru_gate_b, lru_lambda=lru_lambda, lru_w_out=lru_w_out,
             rwkv_mu=rwkv_mu, rwkv_w_r=rwkv_w_r, rwkv_w_k=rwkv_w_k, rwkv_w_v=rwkv_w_v,
             rwkv_w_g=rwkv_w_g, rwkv_w_o=rwkv_w_o, rwkv_w0=rwkv_w0, rwkv_w1=rwkv_w1, rwkv_w2=rwkv_w2,
             rwkv_a0=rwkv_a0, rwkv_a1=rwkv_a1, rwkv_a2=rwkv_a2, rwkv_k_k=rwkv_k_k, rwkv_k_a=rwkv_k_a,
             rwkv_r_k=rwkv_r_k, rwkv_ln_w=rwkv_ln_w, rwkv_ln_b=rwkv_ln_b,
             rwkv_v0=rwkv_v0, rwkv_v1=rwkv_v1, rwkv_v2=rwkv_v2)
    bp = x_prompt.shape[0]
    zero_lru = jnp.zeros((N_LRU, 2, bp, D_INNER), jnp.float32)
    zero_rwkv = jnp.zeros((N_RWKV, 2, bp, RWKV_HEADS, RWKV_HEAD, RWKV_HEAD), jnp.float32)
    y_prompt, fin_lru, fin_rwkv = trunk(x_prompt, c_ctx[None, :], zero_lru, zero_rwkv, shift_sequence, P)
    new_state_lru = jnp.moveaxis(fin_lru, 2, 0)
    new_state_rwkv = jnp.moveaxis(fin_rwkv, 2, 0)
    init_lru = jnp.moveaxis(state_lru, 0, 2)
    init_rwkv = jnp.moveaxis(state_rwkv, 0, 2)
    y_sample, _, _ = trunk(x_sample, c, init_lru, init_rwkv, shift_grid, P)
    return (y_prompt, y_sample, new_state_lru, new_state_rwkv)
```

```python
import contextlib
import numpy as np
import concourse.bass as bass
import concourse.mybir as mybir
from concourse.bass_utils import run_bass_kernel_spmd

F32 = mybir.dt.float32
BF16 = mybir.dt.bfloat16
AF = mybir.ActivationFunctionType
ALU = mybir.AluOpType
AX = mybir.AxisListType

D = 2048
E = 4096
DC = D // 128
EC = E // 128
DEPTH = 4
NH = 64
NORM_EPS = 1e-6
GN_EPS = 64e-5
LRU_C = 8.0
N_DMA_SLOTS = 8


class Op:
    __slots__ = ("eng", "fn", "rd", "wr", "dma", "deps", "sig", "slot", "slot_val", "idx")

    def __init__(self, eng, fn, rd, wr, dma):
        self.eng = eng
        self.fn = fn
        self.rd = rd
        self.wr = wr
        self.dma = dma
        self.deps = ()
        self.sig = 0
        self.slot = -1
        self.slot_val = 0


class Prog:
    ENGS = ("pe", "act", "dve", "pool", "sp")

    def __init__(self, nc, same_engine_sync=True):
        self.nc = nc
        self.ops = []
        self.same_engine_sync = same_engine_sync

    def add(self, eng, fn, rd=(), wr=(), dma=False):
        op = Op(eng, fn, tuple(rd), tuple(wr), dma)
        self.ops.append(op)
        return op

    def pe(self, fn, rd=(), wr=()):
        return self.add("pe", fn, rd, wr)

    def act(self, fn, rd=(), wr=()):
        return self.add("act", fn, rd, wr)

    def dve(self, fn, rd=(), wr=()):
        return self.add("dve", fn, rd, wr)

    def pool(self, fn, rd=(), wr=()):
        return self.add("pool", fn, rd, wr)

    def dma(self, q, out, in_, rd=(), wr=(), **kw):
        return self.add(q, lambda e: e.dma_start(out=out, in_=in_, **kw), rd, wr, dma=True)

    def emit(self):
        nc = self.nc
        ops = self.ops
        last_w = {}
        readers = {}
        for i, op in enumerate(ops):
            op.idx = i
            deps = set()
            for k in op.rd:
                w = last_w.get(k)
                if w is not None:
                    deps.add(w)
            for k in op.wr:
                w = last_w.get(k)
                if w is not None:
                    deps.add(w)
                r = readers.get(k)
                if r:
                    deps.update(r[0].values())
                    deps.update(r[1])
            deps.discard(i)
            for k in op.rd:
                r = readers.get(k)
                if r is None:
                    r = readers[k] = ({}, [])
                if op.dma:
                    r[1].append(i)
                else:
                    r[0][op.eng] = i
            for k in op.wr:
                last_w[k] = i
                readers[k] = ({}, [])
            dl = []
            for j in deps:
                oj = ops[j]
                if (not oj.dma) and oj.eng == op.eng:
                    if op.eng == "pe" or not self.same_engine_sync:
                        continue
                dl.append(j)
            op.deps = dl
        needed = set()
        for op in ops:
            needed.update(op.deps)
        if not hasattr(self, "cnt"):
            self.cnt = {e: 0 for e in self.ENGS}
            self.slot_rr = {e: 0 for e in self.ENGS}
            self.slot_tot = {e: [0] * N_DMA_SLOTS for e in self.ENGS}
            self.sem_e = {e: nc.alloc_semaphore(name="s_" + e) for e in ("pe", "act", "dve", "pool")}
            self.sem_d = {e: [nc.alloc_semaphore(name="d_%s%d" % (e, s)) for s in range(N_DMA_SLOTS)]
                          for e in ("sp", "pool", "act")}
        cnt, slot_rr, slot_tot = self.cnt, self.slot_rr, self.slot_tot
        for op in ops:
            if op.dma:
                s = slot_rr[op.eng]
                slot_rr[op.eng] = (s + 1) % N_DMA_SLOTS
                op.slot = s
                slot_tot[op.eng][s] += 16
                op.slot_val = slot_tot[op.eng][s]
            elif op.idx in needed:
                cnt[op.eng] += 1
                op.sig = cnt[op.eng]
        stack = contextlib.ExitStack()
        sem_e = self.sem_e
        sem_d = self.sem_d
        block = stack.enter_context(nc.Block())
        per_eng = {e: [op for op in ops if op.eng == e] for e in self.ENGS}

        def make(ename):
            def body(eng):
                waited = {}
                for op in per_eng[ename]:
                    for j in op.deps:
                        oj = ops[j]
                        if oj.dma:
                            sem = sem_d[oj.eng][oj.slot]
                            val = oj.slot_val
                        else:
                            sem = sem_e[oj.eng]
                            val = oj.sig
                        key = id(sem)
                        if waited.get(key, 0) >= val:
                            continue
                        waited[key] = val
                        eng.wait_ge(sem, val)
                    if op.dma:
                        sem = sem_d[ename][op.slot]
                        prev = op.slot_val - 16
                        if prev > 0 and waited.get(id(sem), 0) < prev:
                            eng.wait_ge(sem, prev)
                            waited[id(sem)] = prev
                        op.fn(eng).then_inc(sem, 16)
                    else:
                        ins = op.fn(eng)
                        if op.sig:
                            ins.then_inc(sem_e[ename], 1)
                if ename in sem_d:
                    for s in range(N_DMA_SLOTS):
                        tot = slot_tot[ename][s]
                        if tot > 0 and waited.get(id(sem_d[ename][s]), 0) < tot:
                            eng.wait_ge(sem_d[ename][s], tot)
            return body

        if per_eng["sp"]:
            block.sync(make("sp"))
        if per_eng["pe"]:
            block.tensor(make("pe"))
        if per_eng["act"]:
            block.scalar(make("act"))
        if per_eng["dve"]:
            block.vector(make("dve"))
        if per_eng["pool"]:
            block.gpsimd(make("pool"))
        stack.close()
        self.ops = []


def _pm(v):
    v = np.asarray(v, np.float32)
    return np.ascontiguousarray(v.reshape(-1, 128).T)


class PV:
    def __init__(self):
        self.cols = []
        self.off = {}
        self.n = 0

    def put(self, name, v):
        m = _pm(v)
        self.off[name] = self.n
        self.cols.append(m)
        self.n += m.shape[1]

    def arr(self):
        return np.ascontiguousarray(np.concatenate(self.cols, axis=1))


def make_masks(T, L, grid):
    t = np.arange(T)
    p = t % L
    m = np.zeros((12, T), np.float32)
    m[0] = p >= 2
    m[1] = p >= 1
    m[2] = p <= L - 2
    m[3] = ~((p == 0) & (t > 0))
    m[4] = ~((p == L - 1) & (t < T - 1))
    if grid:
        col = t % 64
        row = (t % L) // 64
        nrow = L // 64
        m[5] = col > 0
        m[6] = col < 63
        m[7] = 0
        m[8] = row > 0
        m[9] = 0
        m[10] = row < nrow - 1
        m[11] = 0
    else:
        m[5] = p > 0
        m[6] = 0
        m[7] = p > 0
        m[8] = 0
        m[9] = p < L - 1
        m[10] = 0
        m[11] = p < L - 1
    return m


def make_consts2():
    c = np.zeros((128, 1024), np.float32)
    s = np.arange(128)
    S, Tt = s[:, None], s[None, :]
    blk = lambda b: (S // b) == (Tt // b)
    up, lo = S < Tt, S > Tt
    c[:, 0:128] = blk(8) & up
    c[:, 128:256] = blk(8) & lo
    for l, b in enumerate((16, 32, 64)):
        off = blk(b) & ~blk(b // 2)
        c[:, 256 + l * 256:384 + l * 256] = off & up
        c[:, 384 + l * 256:512 + l * 256] = off & lo
    return c


def make_consts():
    c = np.zeros((128, 1024), np.float32)
    c[:, 0:128] = np.eye(128)
    c[:, 128:256] = 1.0
    blk = np.zeros((128, 128), np.float32)
    blk[:64, :64] = 1
    blk[64:, 64:] = 1
    c[:, 256:384] = blk
    s = np.arange(128)
    same = (s[:, None] // 64) == (s[None, :] // 64)
    c[:, 384:512] = same & (s[:, None] < s[None, :])
    c[:, 512:640] = same & (s[:, None] <= s[None, :])
    c[:, 640:768] = same & (s[:, None] > s[None, :])
    c[:, 768:896] = same & (s[:, None] >= s[None, :])
    c[:64, 896:960] = np.eye(64)
    c[64:, 896:960] = np.eye(64)
    c[:, 960] = NORM_EPS
    c[:, 961] = GN_EPS
    return c


class Builder:
    def __init__(self, T, pv_off, pv_n, depth=DEPTH, dbg=False):
        self.T = T
        self.pv_off = pv_off
        self.pv_n = pv_n
        self.depth = depth
        self.NT = T // 512
        self.dbg = dbg
        nc = self.nc = bass.Bass("TRN2", target_bir_lowering=False)
        self.P = Prog(nc)
        di = lambda n, s, dt=F32: nc.dram_tensor(n, list(s), dt, kind="ExternalInput").ap()
        do = lambda n, s, dt=F32: nc.dram_tensor(n, list(s), dt, kind="ExternalOutput").ap()
        dx = lambda n, s, dt=F32: nc.dram_tensor(n, list(s), dt, kind=("ExternalOutput" if dbg else "Internal")).ap()
        self.nseq = T // 256
        self.xin = di("xin", [T, D])
        self.pvec = di("pvec", [128, pv_n])
        self.modd = dx("modd", [128, 4 * 80])
        self.masks = di("masks", [12, T])
        self.consts = di("consts", [128, 1024])
        self.consts2 = di("consts2", [128, 1024])
        self.lru_h0 = di("lru_h0", [128, 2 * 2 * EC])
        self.w = {}
        for n, s in WEIGHT_SHAPES.items():
            self.w[n] = di(n, s)
        self.yout = do("yout", [T, D])
        self.out_lru = do("out_lru", [self.nseq, 2, 2, E])
        self.xT = dx("xT", [D, T])
        self.xs = dx("xs", [E, T])
        self.zs = dx("zs", [E, T], BF16)
        self.ygs = dx("ygs", [E, T], BF16)
        self.hs = dx("hs", [D, T], BF16)
        self.rs = dx("rs", [E, T], BF16)
        self.ks = dx("ks", [E, T])
        self.vs = [dx("vs0", [E, T]), dx("vs1", [E, T])]
        self.vmx = dx("vmx", [E, T])
        self.wl = [dx("wl0", [E, T]), dx("wl1", [E, T])]
        self.al = [dx("al0", [E, T]), dx("al1", [E, T])]
        self.yf = dx("yf", [EC, T // 64, 128, 66])
        self.w_lnw = di("rwkv_ln_w", [2, E])
        self.w_lnb = di("rwkv_ln_b", [2, E])
        self.rwkv_s0 = di("rwkv_s0", [2, 2, EC, 128, 64])
        self.ckeep = di("ckeep", [128, 2, T // 64])
        self.out_rwkv = do("out_rwkv", [self.nseq, 2, 2, EC, 128, 64])
        self.dx = dx
        self.di = di
        self.do = do

    @contextlib.contextmanager
    def phase(self):
        st = contextlib.ExitStack()
        self._st = st
        self._rot = {}
        self._ph = getattr(self, "_ph", 0) + 1
        try:
            yield st
            self.P.emit()
        finally:
            st.close()

    def sb(self, name, shape, dt=F32):
        return self._st.enter_context(self.nc.sbuf_tensor("p%d_%s" % (self._ph, name), list(shape), dt))

    def ps(self, name, shape, dt=F32):
        return self._st.enter_context(self.nc.psum_tensor("p%d_%s" % (self._ph, name), list(shape), dt))

    def rot(self, name, n):
        v = self._rot.get(name, 0)
        self._rot[name] = v + 1
        return v % n

    def load_common(self, want_masks=()):
        P = self.P
        self.pv = self.sb("pv", [128, self.pv_n])
        P.dma("sp", self.pv[:], self.pvec, wr=["pv"])
        self.cst = self.sb("cst", [128, 1024])
        P.dma("sp", self.cst[:], self.consts, wr=["cst"])
        self.ident = self.cst[:, 0:128]
        self.ones = self.cst[:, 128:256]
        self.mod = self.sb("mod", [128, 4 * 80])
        P.dma("sp", self.mod[:], self.modd, wr=["mod"], rd=["modd"])

    def pcol(self, name, c, n=1):
        o = self.pv_off[name] + c
        return self.pv[:, o:o + n]

    def phase_adaln(self):
        P = self.P
        nc = self.nc
        with self.phase():
            pv = self.sb("pv", [128, self.pv_n])
            P.dma("sp", pv[:], self.pvec, wr=["pv"])
            sc = self.sb("sc", [128, DC], BF16)
            co = self.pv_off["cond"]
            P.act(lambda e: e.activation(sc[:], pv[:, co:co + DC], AF.Silu), rd=["pv"], wr=["sc"])
            wb = [self.sb("wb%d" % s, [128, DC, 512], BF16) for s in range(2)]
            mps = self.ps("mps", [128, 4 * 48])
            mod = self.sb("mod", [128, 4 * 80])
            for i in range(self.depth):
                wv = self.w["ada_w"][i].rearrange("(kc p) o -> p kc o", p=128)
                for og in range(12):
                    s = self.rot("wb", 2)
                    P.dma("pool", wb[s][:], wv[:, :, og * 512:(og + 1) * 512], wr=["wb%d" % s])
                    for oc in range(4):
                        col = i * 48 + og * 4 + oc
                        for kc in range(DC):
                            P.pe(lambda e, s=s, kc=kc, oc=oc, col=col: e.matmul(
                                mps[:, col:col + 1], wb[s][:, kc, oc * 128:(oc + 1) * 128], sc[:, kc:kc + 1],
                                start=(kc == 0), stop=(kc == DC - 1)), rd=["wb%d" % s, "sc"], wr=["mps"])
                ab = self.pv_off["ada_b%d" % i]
                P.dve(lambda e, i=i, ab=ab: e.tensor_tensor(mod[:, i * 80:i * 80 + 48], mps[:, i * 48:(i + 1) * 48],
                                                             pv[:, ab:ab + 48], ALU.add), rd=["mps", "pv"], wr=["mod"])
                npre = self.pv_off["norm_pre%d" % i]
                npost = self.pv_off["norm_post%d" % i]
                P.dve(lambda e, i=i, npre=npre: e.scalar_tensor_tensor(
                    mod[:, i * 80 + 48:i * 80 + 64], mod[:, i * 80 + 16:i * 80 + 32], 1.0, pv[:, npre:npre + 16],
                    ALU.add, ALU.mult), rd=["mod", "pv"], wr=["mod"])
                P.dve(lambda e, i=i, npost=npost: e.tensor_tensor(
                    mod[:, i * 80 + 64:i * 80 + 80], mod[:, i * 80 + 32:i * 80 + 48], pv[:, npost:npost + 16],
                    ALU.mult), rd=["mod", "pv"], wr=["mod"])
            P.dma("sp", self.modd, mod[:], rd=["mod"], wr=["modd"])

    def phase_t0(self):
        P = self.P
        with self.phase():
            self.load_common()
            xt = [self.sb("xt%d" % s, [128, 4, D]) for s in range(1)]
            xo = [self.sb("xo%d" % s, [128, DC, 512]) for s in range(2)]
            tp = [self.ps("tp%d" % s, [128, 512]) for s in range(4)]
            xv = self.xin.rearrange("(tt s p) d -> tt p s d", s=4, p=128)
            xTv = self.xT.rearrange("(c p) t -> p c t", p=128)
            for tt in range(self.NT):
                P.dma("sp", xt[0][:], xv[tt], wr=["xt0"])
                so = self.rot("xo", 2)
                for c in range(DC):
                    ts_ = self.rot("tp", 4)
                    for s in range(4):
                        P.pe(lambda e, c=c, s=s, ts_=ts_: e.transpose(
                            tp[ts_][:, s * 128:(s + 1) * 128], xt[0][:, s, c * 128:(c + 1) * 128], self.ident),
                            rd=["xt0", "cst"], wr=["tp%d" % ts_])
                    ev = P.act if c % 2 == 0 else P.dve
                    if c % 2 == 0:
                        P.act(lambda e, c=c, so=so, ts_=ts_: e.copy(xo[so][:, c, :], tp[ts_][:]),
                              rd=["tp%d" % ts_], wr=[("xo", so, c)])
                    else:
                        P.dve(lambda e, c=c, so=so, ts_=ts_: e.tensor_copy(xo[so][:, c, :], tp[ts_][:]),
                              rd=["tp%d" % ts_], wr=[("xo", so, c)])
                P.dma("sp", xTv[:, :, tt * 512:(tt + 1) * 512], xo[so][:],
                      rd=[("xo", so, c) for c in range(DC)], wr=[("xT", tt)])

    def phase_tf(self):
        P = self.P
        with self.phase():
            self.load_common()
            xi = [self.sb("xi%d" % s, [128, DC, 512]) for s in range(1)]
            yo = [self.sb("yo%d" % s, [128, 4, D]) for s in range(2)]
            tp = [self.ps("tp%d" % s, [128, 512]) for s in range(4)]
            yv = self.yout.rearrange("(tt s p) d -> tt p s d", s=4, p=128)
            xTv = self.xT.rearrange("(c p) t -> p c t", p=128)
            for tt in range(self.NT):
                P.dma("sp", xi[0][:], xTv[:, :, tt * 512:(tt + 1) * 512], rd=[("xT", tt)], wr=["xi0"])
                so = self.rot("yo", 2)
                k = 0
                for s in range(4):
                    for cg in range(4):
                        ts_ = self.rot("tp", 4)
                        for c4 in range(4):
                            c = cg * 4 + c4
                            P.pe(lambda e, c=c, c4=c4, s=s, ts_=ts_: e.transpose(
                                tp[ts_][:, c4 * 128:(c4 + 1) * 128], xi[0][:, c, s * 128:(s + 1) * 128], self.ident),
                                rd=["xi0", "cst"], wr=["tp%d" % ts_])
                        if k % 2 == 0:
                            P.act(lambda e, s=s, cg=cg, so=so, ts_=ts_: e.copy(
                                yo[so][:, s, cg * 512:(cg + 1) * 512], tp[ts_][:]),
                                rd=["tp%d" % ts_], wr=[("yo", so, s, cg)])
                        else:
                            P.dve(lambda e, s=s, cg=cg, so=so, ts_=ts_: e.tensor_copy(
                                yo[so][:, s, cg * 512:(cg + 1) * 512], tp[ts_][:]),
                                rd=["tp%d" % ts_], wr=[("yo", so, s, cg)])
                        k += 1
                P.dma("sp", yv[tt], yo[so][:], rd=[("yo", so, s, cg) for s in range(4) for cg in range(4)])

    def pre_setup(self):
        self.pre_xt = [self.sb("pxt%d" % s, [128, DC, 512]) for s in range(2)]
        self.pre_sq = [self.sb("psq%d" % s, [128, 512]) for s in range(4)]
        self.pre_ss = self.ps("pss", [128, 512])
        self.pre_r = [self.sb("prs%d" % s, [128, 512]) for s in range(2)]

    def pre_tile(self, i, tt, dst, dkey):
        P = self.P
        xTv = self.xT.rearrange("(c p) t -> p c t", p=128)
        s = self.rot("pxt", 2)
        xt = self.pre_xt[s]
        xk = "pxt%d" % s
        P.dma("sp", xt[:], xTv[:, :, tt * 512:(tt + 1) * 512], rd=[("xT", tt)], wr=[xk])
        for c in range(DC):
            q = self.rot("psq", 4)
            P.act(lambda e, c=c, q=q: e.activation(self.pre_sq[q][:], xt[:, c, :], AF.Square),
                  rd=[xk], wr=["psq%d" % q])
            P.pe(lambda e, c=c, q=q: e.matmul(self.pre_ss[:], self.ones, self.pre_sq[q][:],
                                               start=(c == 0), stop=(c == DC - 1)),
                 rd=["psq%d" % q, "cst"], wr=["pss"])
        r = self.rot("prs", 2)
        rs = self.pre_r[r]
        rk = "prs%d" % r
        P.act(lambda e: e.activation(rs[:], self.pre_ss[:], AF.Sqrt, bias=self.eps_ap, scale=1.0 / D),
              rd=["pss", "cst"], wr=[rk])
        P.dve(lambda e: e.reciprocal(rs[:], rs[:]), rd=[rk], wr=[rk])
        mo = i * 80
        for c in range(DC):
            q = self.rot("psq", 4)
            P.dve(lambda e, c=c, q=q: e.scalar_tensor_tensor(
                self.pre_sq[q][:], xt[:, c, :], self.mod[:, mo + 48 + c:mo + 49 + c], rs[:], ALU.mult, ALU.mult),
                rd=[xk, rk, "mod"], wr=["psq%d" % q])
            P.act(lambda e, c=c, q=q: e.activation(dst(c), self.pre_sq[q][:], AF.Identity,
                                                    bias=self.mod[:, mo + c:mo + c + 1], scale=1.0),
                  rd=["psq%d" % q, "mod"], wr=[dkey(c)])

    def post_setup(self):
        self.po_x = [self.sb("pox%d" % s, [128, DC, 512]) for s in range(1)]
        self.po_sq = [self.sb("posq%d" % s, [128, 512]) for s in range(4)]
        self.po_ss = self.ps("poss", [128, 512])
        self.po_r = [self.sb("pors%d" % s, [128, 512]) for s in range(2)]

    def post_tile(self, i, tt, src, skey):
        P = self.P
        xTv = self.xT.rearrange("(c p) t -> p c t", p=128)
        xt = self.po_x[0]
        P.dma("sp", xt[:], xTv[:, :, tt * 512:(tt + 1) * 512], rd=[("xT", tt)], wr=["pox0"])
        for c in range(DC):
            q = self.rot("posq", 4)
            P.act(lambda e, c=c, q=q: e.activation(self.po_sq[q][:], src(c), AF.Square),
                  rd=[skey(c)], wr=["posq%d" % q])
            P.pe(lambda e, c=c, q=q: e.matmul(self.po_ss[:], self.ones, self.po_sq[q][:],
                                               start=(c == 0), stop=(c == DC - 1)),
                 rd=["posq%d" % q, "cst"], wr=["poss"])
        r = self.rot("pors", 2)
        rs = self.po_r[r]
        rk = "pors%d" % r
        P.act(lambda e: e.activation(rs[:], self.po_ss[:], AF.Sqrt, bias=self.eps_ap, scale=1.0 / D),
              rd=["poss", "cst"], wr=[rk])
        P.dve(lambda e: e.reciprocal(rs[:], rs[:]), rd=[rk], wr=[rk])
        mo = i * 80
        for c in range(DC):
            q = self.rot("posq", 4)
            P.dve(lambda e, c=c, q=q: e.scalar_tensor_tensor(
                self.po_sq[q][:], src(c), self.mod[:, mo + 64 + c:mo + 65 + c], rs[:], ALU.mult, ALU.mult),
                rd=[skey(c), rk, "mod"], wr=["posq%d" % q])
            P.dve(lambda e, c=c, q=q: e.tensor_tensor(xt[:, c, :], xt[:, c, :], self.po_sq[q][:], ALU.add),
                  rd=["posq%d" % q, "pox0"], wr=["pox0"])
        P.dma("sp", xTv[:, :, tt * 512:(tt + 1) * 512], xt[:], rd=["pox0"], wr=[("xT", tt)])

    def phase_outproj(self, i, wname, j):
        P = self.P
        T = self.T
        TB = min(1024, T)
        with self.phase():
            self.load_common()
            self.eps_ap = self.cst[:, 960:961]
            self.post_setup()
            yg = self.sb("yg", [128, EC, TB], BF16)
            mT = self.sb("mT", [128, DC, TB])
            wb = [self.sb("wo%d" % s, [128, EC, 128], BF16) for s in range(2)]
            acc = [self.ps("acc%d" % s, [128, 512]) for s in range(4)]
            ygv = self.ygs.rearrange("(c p) t -> p c t", p=128)
            wv = self.w[wname][j].rearrange("(kc p) o -> p kc o", p=128)
            for tb in range(T // TB):
                P.dma("sp", yg[:], ygv[:, :, tb * TB:(tb + 1) * TB], rd=[("ygs", tb)], wr=["yg"])
                for oc in range(DC):
                    s = self.rot("wo", 2)
                    P.dma("pool", wb[s][:], wv[:, :, oc * 128:(oc + 1) * 128], wr=["wo%d" % s])
                    for t2 in range(TB // 512):
                        a = self.rot("acc", 4)
                        for kc in range(EC):
                            P.pe(lambda e, s=s, kc=kc, t2=t2, a=a: e.matmul(
                                acc[a][:], wb[s][:, kc, :], yg[:, kc, t2 * 512:(t2 + 1) * 512],
                                start=(kc == 0), stop=(kc == EC - 1)), rd=["wo%d" % s, "yg"], wr=["acc%d" % a])
                        if (oc + t2) % 2 == 0:
                            P.act(lambda e, oc=oc, t2=t2, a=a: e.copy(mT[:, oc, t2 * 512:(t2 + 1) * 512], acc[a][:]),
                                  rd=["acc%d" % a], wr=[("mT", oc, t2)])
                        else:
                            P.dve(lambda e, oc=oc, t2=t2, a=a: e.tensor_copy(mT[:, oc, t2 * 512:(t2 + 1) * 512], acc[a][:]),
                                  rd=["acc%d" % a], wr=[("mT", oc, t2)])
                for t2 in range(TB // 512):
                    tt = tb * (TB // 512) + t2
                    self.post_tile(i, tt, lambda c, t2=t2: mT[:, c, t2 * 512:(t2 + 1) * 512],
                                   lambda c, t2=t2: ("mT", c, t2))

    def phase_lru1(self, i, j):
        P = self.P
        T = self.T
        TB = min(2048, T)
        n4 = TB // 512
        with self.phase():
            self.load_common()
            self.eps_ap = self.cst[:, 960:961]
            self.pre_setup()
            hT = self.sb("hT", [128, DC, TB], BF16)
            wb = [self.sb("wb%d" % s, [128, DC, 512], BF16) for s in range(2)]
            sx = [self.sb("sx%d" % s, [128, 512]) for s in range(3)]
            sz = [self.sb("sz%d" % s, [128, 512], BF16) for s in range(3)]
            acc = [self.ps("acc%d" % s, [128, 512]) for s in range(4)]
            wv = self.w["lru_w_in"][j].rearrange("(kc p) o -> p kc o", p=128)
            for tb in range(T // TB):
                for t4 in range(n4):
                    self.pre_tile(i, tb * n4 + t4,
                                  lambda c, t4=t4: hT[:, c, t4 * 512:(t4 + 1) * 512],
                                  lambda c, t4=t4: ("hT", t4))
                for og in range(16):
                    s = self.rot("wb", 2)
                    P.dma("pool", wb[s][:], wv[:, :, og * 512:(og + 1) * 512], wr=["wb%d" % s])
                    for oc in range(4):
                        ec = (og % 8) * 4 + oc
                        for t4 in range(n4):
                            a = self.rot("acc", 4)
                            for kc in range(DC):
                                P.pe(lambda e, s=s, kc=kc, oc=oc, t4=t4, a=a: e.matmul(
                                    acc[a][:], wb[s][:, kc, oc * 128:(oc + 1) * 128],
                                    hT[:, kc, t4 * 512:(t4 + 1) * 512],
                                    start=(kc == 0), stop=(kc == DC - 1)),
                                    rd=["wb%d" % s, ("hT", t4)], wr=["acc%d" % a])
                            t0 = tb * TB + t4 * 512
                            if og < 8:
                                q = self.rot("sx", 3)
                                P.dve(lambda e, q=q, a=a: e.tensor_copy(sx[q][:], acc[a][:]),
                                      rd=["acc%d" % a], wr=["sx%d" % q])
                                P.dma("sp", self.xs[ec * 128:(ec + 1) * 128, t0:t0 + 512], sx[q][:],
                                      rd=["sx%d" % q], wr=[("xs", ec)])
                            else:
                                q = self.rot("sz", 3)
                                P.act(lambda e, q=q, a=a: e.activation(sz[q][:], acc[a][:], AF.Silu),
                                      rd=["acc%d" % a], wr=["sz%d" % q])
                                P.dma("sp", self.zs[ec * 128:(ec + 1) * 128, t0:t0 + 512], sz[q][:],
                                      rd=["sz%d" % q], wr=[("zs", ec)])

    def phase_lru2(self, i, j):
        P = self.P
        T = self.T
        SBK = min(1024, T)
        nsb = T // SBK
        nseq = self.nseq
        with self.phase():
            self.load_common()
            mk = self.sb("mk", [128, 5, T], BF16)
            for m in range(5):
                P.dma("pool", mk[:, m, :], self.masks[m:m + 1, :].partition_broadcast(128), wr=[("mk", m)])
            h0 = self.sb("h0", [128, 2 * 2 * EC])
            P.dma("sp", h0[:], self.lru_h0, wr=["h0"])
            fin = self.sb("fin", [128, 2, nseq, EC])
            sp8 = self.sb("sp8", [128, 2, EC])
            sp16 = self.sb("sp16", [128, 2, EC])
            lo = self.pv_off["lru_lambda%d" % j]
            P.act(lambda e: e.activation(sp8[:].rearrange("p a b -> p (a b)"), self.pv[:, lo:lo + 2 * EC], AF.Exp, scale=-1.0),
                  rd=["pv"], wr=["sp8"])
            P.act(lambda e: e.activation(sp8[:].rearrange("p a b -> p (a b)"), sp8[:].rearrange("p a b -> p (a b)"), AF.Ln, bias=1.0),
                  rd=["sp8"], wr=["sp8"])
            P.dve(lambda e: e.tensor_scalar(sp16[:].rearrange("p a b -> p (a b)"), sp8[:].rearrange("p a b -> p (a b)"), -2.0 * LRU_C, None, ALU.mult),
                  rd=["sp8"], wr=["sp16"])
            P.dve(lambda e: e.tensor_scalar(sp8[:].rearrange("p a b -> p (a b)"), sp8[:].rearrange("p a b -> p (a b)"), -LRU_C, None, ALU.mult),
                  rd=["sp8"], wr=["sp8"])
            xr = self.sb("xr", [128, 2, T + 4])
            P.pool(lambda e: e.memset(xr[:, :, 0:2], 0.0), wr=["xr_pad"])
            P.pool(lambda e: e.memset(xr[:, :, T + 2:T + 4], 0.0), wr=["xr_pad"])
            xc = self.sb("xc", [128, 2, T])
            xcb = self.sb("xcb", [128, 2, T], BF16)
            tmp = [self.sb("ctmp%d" % s, [128, SBK]) for s in range(2)]
            A = [self.sb("A%d" % s, [128, SBK]) for s in range(2)]
            Bb = [self.sb("B%d" % s, [128, SBK]) for s in range(2)]
            C = [self.sb("C%d" % s, [128, SBK]) for s in range(2)]
            carry = self.sb("carry", [128, 2])
            zt = [self.sb("zt%d" % s, [128, T], BF16) for s in range(2)]
            ygo = [self.sb("ygo%d" % s, [128, T], BF16) for s in range(2)]
            gw = [self.sb("gw%d" % s, [128, 2, 256], BF16) for s in range(2)]
            acc = [self.ps("acc%d" % s, [128, 512]) for s in range(6)]
            cw = self.pv_off["lru_conv_w%d" % j]
            cb = self.pv_off["lru_conv_b%d" % j]
            gbo = self.pv_off["lru_gate_b%d" % j]
            for nb in range(16):
                for o in range(2):
                    ec = nb * 2 + o
                    P.dma("sp", xr[:, o, 2:T + 2], self.xs[ec * 128:(ec + 1) * 128, :], rd=[("xs", ec)], wr=[("xr", o)])
                for o in range(2):
                    ec = nb * 2 + o
                    for sbk in range(nsb):
                        t0 = sbk * SBK
                        xk = ("xc", o, sbk)
                        P.dve(lambda e, o=o, t0=t0, ec=ec: e.tensor_scalar(
                            xc[:, o, t0:t0 + SBK], xr[:, o, 2 + t0:2 + t0 + SBK],
                            self.pv[:, cw + 2 * EC + ec:cw + 2 * EC + ec + 1], self.pv[:, cb + ec:cb + ec + 1],
                            ALU.mult, ALU.add), rd=[("xr", o), "xr_pad", "pv"], wr=[xk])
                        for (tap, off, m) in ((0, -2, 0), (1, -1, 1), (3, 1, 2)):
                            q = self.rot("ctmp", 2)
                            P.pool(lambda e, o=o, t0=t0, off=off, m=m, q=q: e.tensor_tensor(
                                tmp[q][:], xr[:, o, 2 + t0 + off:2 + t0 + off + SBK], mk[:, m, t0:t0 + SBK], ALU.mult),
                                rd=[("xr", o), "xr_pad", ("mk", m)], wr=["ctmp%d" % q])
                            P.dve(lambda e, o=o, t0=t0, tap=tap, q=q, ec=ec: e.scalar_tensor_tensor(
                                xc[:, o, t0:t0 + SBK], tmp[q][:], self.pv[:, cw + tap * EC + ec:cw + tap * EC + ec + 1],
                                xc[:, o, t0:t0 + SBK], ALU.mult, ALU.add), rd=["ctmp%d" % q, xk, "pv"], wr=[xk])
                        P.act(lambda e, o=o, t0=t0: e.copy(xcb[:, o, t0:t0 + SBK], xc[:, o, t0:t0 + SBK]),
                              rd=[xk], wr=[("xcb", o, sbk)])
                for o in range(2):
                    ec = nb * 2 + o
                    P.dma("sp", zt[o][:], self.zs[ec * 128:(ec + 1) * 128, :], rd=[("zs", ec)], wr=[("zt", o)])
                for d in range(2):
                    gws = []
                    for g in range(2):
                        s = self.rot("gw", 2)
                        P.dma("pool", gw[s][:], self.w["lru_gate_w"][j, d, g, nb].rearrange("(kc p) o -> p kc o", p=128),
                              wr=["gw%d" % s])
                        gws.append(s)
                    order = list(range(nsb)) if d == 0 else list(range(nsb - 1, -1, -1))
                    for o in range(2):
                        ec = nb * 2 + o
                        for si, sbk in enumerate(order):
                            t0 = sbk * SBK
                            qa = self.rot("A", 2)
                            qb = self.rot("B", 2)
                            qc = self.rot("C", 2)
                            for g in range(2):
                                dst = A[qa] if g == 0 else Bb[qb]
                                dk = ("A%d" % qa) if g == 0 else ("B%d" % qb)
                                gb_col = gbo + (d * 2 + g) * EC + ec
                                for t5 in range(SBK // 512):
                                    a = self.rot("acc", 6)
                                    for kc in range(2):
                                        P.pe(lambda e, g=g, kc=kc, o=o, t0=t0, t5=t5, a=a: e.matmul(
                                            acc[a][:], gw[gws[g]][:, kc, o * 128:(o + 1) * 128],
                                            xcb[:, kc, t0 + t5 * 512:t0 + (t5 + 1) * 512],
                                            start=(kc == 0), stop=(kc == 1)),
                                            rd=["gw%d" % gws[g], ("xcb", 0, sbk), ("xcb", 1, sbk)], wr=["acc%d" % a])
                                    P.act(lambda e, dst=dst, t5=t5, a=a, gb_col=gb_col: e.activation(
                                        dst[:, t5 * 512:(t5 + 1) * 512], acc[a][:], AF.Sigmoid,
                                        bias=self.pv[:, gb_col:gb_col + 1], scale=1.0),
                                        rd=["acc%d" % a, "pv"], wr=[dk])
                            ak, bk, ck = "A%d" % qa, "B%d" % qb, "C%d" % qc
                            Aq, Bq, Cq = A[qa], Bb[qb], C[qc]
                            P.act(lambda e, Aq=Aq, Cq=Cq, d=d, ec=ec: e.activation(
                                Cq[:], Aq[:], AF.Exp, scale=sp16[:, d, ec:ec + 1]), rd=[ak, "sp16"], wr=[ck])
                            P.act(lambda e, Aq=Aq, d=d, ec=ec: e.activation(
                                Aq[:], Aq[:], AF.Exp, scale=sp8[:, d, ec:ec + 1]), rd=[ak, "sp8"], wr=[ak])
                            P.act(lambda e, Cq=Cq: e.activation(Cq[:], Cq[:], AF.Sqrt, bias=1.0, scale=-1.0),
                                  rd=[ck], wr=[ck])
                            P.dve(lambda e, Bq=Bq, o=o, t0=t0: e.tensor_tensor(Bq[:], Bq[:], xc[:, o, t0:t0 + SBK], ALU.mult),
                                  rd=[bk, ("xc", o, sbk)], wr=[bk])
                            P.dve(lambda e, Bq=Bq, Cq=Cq: e.tensor_tensor(Bq[:], Bq[:], Cq[:], ALU.mult),
                                  rd=[bk, ck], wr=[bk])
                            P.pool(lambda e, Aq=Aq, d=d, t0=t0: e.tensor_tensor(Aq[:], Aq[:], mk[:, 3 + d, t0:t0 + SBK], ALU.mult),
                                   rd=[ak, ("mk", 3 + d)], wr=[ak])
                            yk = ("xr", o)
                            if d == 0:
                                if si == 0:
                                    init = h0[:, (j * 2 + 0) * EC + ec:(j * 2 + 0) * EC + ec + 1]
                                else:
                                    init = xr[:, o, 2 + t0 - 1:2 + t0]
                                P.dve(lambda e, Aq=Aq, Bq=Bq, o=o, t0=t0, init=init: e.tensor_tensor_scan(
                                    xr[:, o, 2 + t0:2 + t0 + SBK], Aq[:], Bq[:], init, ALU.mult, ALU.add),
                                    rd=[ak, bk, yk, "h0"], wr=[yk])
                            else:
                                if si == 0:
                                    init = h0[:, (j * 2 + 1) * EC + ec:(j * 2 + 1) * EC + ec + 1]
                                else:
                                    init = carry[:, o:o + 1]
                                P.dve(lambda e, Aq=Aq, Bq=Bq, Cq=Cq, init=init: e.tensor_tensor_scan(
                                    Cq[:, ::-1], Aq[:, ::-1], Bq[:, ::-1], init, ALU.mult, ALU.add),
                                    rd=[ak, bk, "h0", ("carry", o)], wr=[ck])
                                P.act(lambda e, Cq=Cq, o=o: e.copy(carry[:, o:o + 1], Cq[:, 0:1]),
                                      rd=[ck], wr=[("carry", o)])
                                ns = SBK // 256
                                s0 = t0 // 256
                                P.act(lambda e, Cq=Cq, s0=s0, ns=ns, ec=ec: e.copy(
                                    fin[:, 1, s0:s0 + ns, ec], Cq[:, 0:SBK:256]), rd=[ck], wr=["fin"])
                                P.dve(lambda e, Cq=Cq, o=o, t0=t0: e.tensor_tensor(
                                    xr[:, o, 2 + t0:2 + t0 + SBK], xr[:, o, 2 + t0:2 + t0 + SBK], Cq[:], ALU.add),
                                    rd=[ck, yk], wr=[yk])
                        if d == 0:
                            P.act(lambda e, o=o, ec=ec: e.copy(fin[:, 0, :, ec], xr[:, o, 2 + 255:2 + T:256]),
                                  rd=[("xr", o)], wr=["fin"])
                for o in range(2):
                    ec = nb * 2 + o
                    q = self.rot("ygo", 2)
                    P.dve(lambda e, o=o, q=q: e.tensor_tensor(ygo[q][:], xr[:, o, 2:T + 2], zt[o][:], ALU.mult),
                          rd=[("xr", o), ("zt", o)], wr=["ygo%d" % q])
                    P.dma("sp", self.ygs[ec * 128:(ec + 1) * 128, :], ygo[q][:], rd=["ygo%d" % q],
                          wr=[("ygs", tb) for tb in range(max(1, T // 1024))])
            fo = self.sb("fo", [128, 128])
            tp = self.ps("ftp", [128, 128])
            rows_per = 128 // EC
            for d in range(2):
                for g in range(max(1, nseq // rows_per)):
                    ns = min(rows_per, nseq)
                    n = ns * EC
                    P.pe(lambda e, d=d, g=g, ns=ns, n=n: e.transpose(
                        tp[0:n, :], fin[:, d, g * rows_per:g * rows_per + ns, :].rearrange("p s c -> p (s c)"), self.ident),
                        rd=["fin", "cst"], wr=["ftp"])
                    P.dve(lambda e, n=n: e.tensor_copy(fo[0:n, :], tp[0:n, :]), rd=["ftp"], wr=["fo"])
                    for sl in range(ns):
                        P.dma("sp", self.out_lru[g * rows_per + sl, j, d, :].rearrange("(c p) -> c p", p=128),
                              fo[sl * EC:(sl + 1) * EC, :], rd=["fo"])

    def rwkv_layer(self, i, j):
        import os
        stop = os.environ.get("K_STOP", "")
        self.phase_rw0(i)
        if stop == "rw0":
            return
        self.phase_rw1(i, j)
        if stop == "rw1":
            return
        self.phase_rw2(i, j)
        if stop == "rw2":
            return
        self.phase_outproj(i, "rwkv_w_o", j)

    def phase_rw0(self, i):
        P = self.P
        with self.phase():
            self.load_common()
            self.eps_ap = self.cst[:, 960:961]
            self.pre_setup()
            ho = [self.sb("ho%d" % s, [128, DC, 512], BF16) for s in range(2)]
            hv = self.hs.rearrange("(c p) t -> p c t", p=128)
            for tt in range(self.NT):
                s = self.rot("ho", 2)
                self.pre_tile(i, tt, lambda c, s=s: ho[s][:, c, :], lambda c, s=s: "ho%d" % s)
                P.dma("sp", hv[:, :, tt * 512:(tt + 1) * 512], ho[s][:], rd=["ho%d" % s], wr=[("hs", tt)])

    def phase_rw1(self, i, j):
        P = self.P
        T = self.T
        TB = min(1024, T)
        n2 = TB // 512
        with self.phase():
            self.load_common()
            hh = self.sb("hh", [128, DC, TB + 128], BF16)
            xx = self.sb("xx", [128, DC, TB], BF16)
            xm = [self.sb("xm%d" % s, [128, DC, TB], BF16) for s in range(1)]
            mk = self.sb("mk7", [128, 7, TB], BF16)
            t1s = [self.sb("sh%d" % s, [128, TB]) for s in range(1)]
            t2s = [self.sb("sh2%d" % s, [128, TB]) for s in range(1)]
            wb = [self.sb("wb%d" % s, [128, DC, 512], BF16) for s in range(2)]
            l1 = [self.sb("l1%d" % s, [128, DC, 128], BF16) for s in range(2)]
            l2 = [self.sb("l2%d" % s, [128, E], BF16) for s in range(2)]
            lt = [self.sb("lt%d" % s, [128, 512], BF16) for s in range(2)]
            sf = [self.sb("sf%d" % s, [128, 512]) for s in range(4)]
            sh = [self.sb("sb%d" % s, [128, 512], BF16) for s in range(3)]
            acc = [self.ps("acc%d" % s, [128, 512]) for s in range(6)]
            hv = self.hs.rearrange("(c p) t -> p c t", p=128)
            muo = self.pv_off["rwkv_mu%d" % j]
            terms = {0: [(-1, 0)], 1: [(1, 1), (-1, 2)], 2: [(-64, 3), (1, 4)], 3: [(64, 5), (1, 6)]}
            evk = [0]

            def evac(kind, a, dst_dram, ec, t0):
                if kind in ("bf", "silu"):
                    q = self.rot("sb", 3)
                    if kind == "silu":
                        P.act(lambda e: e.activation(sh[q][:], acc[a][:], AF.Silu), rd=["acc%d" % a], wr=["sb%d" % q])
                    else:
                        P.act(lambda e: e.copy(sh[q][:], acc[a][:]), rd=["acc%d" % a], wr=["sb%d" % q])
                    P.dma("sp", dst_dram[ec * 128:(ec + 1) * 128, t0:t0 + 512], sh[q][:], rd=["sb%d" % q])
                else:
                    q = self.rot("sf", 4)
                    evk[0] += 1
                    if evk[0] % 3 == 0:
                        P.act(lambda e: e.copy(sf[q][:], acc[a][:]), rd=["acc%d" % a], wr=["sf%d" % q])
                    else:
                        P.dve(lambda e: e.tensor_copy(sf[q][:], acc[a][:]), rd=["acc%d" % a], wr=["sf%d" % q])
                    P.dma("sp", dst_dram[ec * 128:(ec + 1) * 128, t0:t0 + 512], sf[q][:], rd=["sf%d" % q])

            def mix(m, tb):
                s = self.rot("xm", 1)
                for c in range(DC):
                    eng = P.dve
                    eng(lambda e, c=c, s=s: e.scalar_tensor_tensor(
                        xm[s][:, c, :], xx[:, c, :], self.pv[:, muo + m * DC + c:muo + m * DC + c + 1],
                        hh[:, c, 64:64 + TB], ALU.mult, ALU.add), rd=["xx", "hh", "pv"], wr=[("xm", s, c)])
                return xm[s], [("xm", s, c) for c in range(DC)]

            def proj(src, skeys, wname, kind, dst, tb):
                wv = self.w[wname][j].rearrange("(kc p) o -> p kc o", p=128)
                for og in range(8):
                    s = self.rot("wb", 2)
                    P.dma("pool", wb[s][:], wv[:, :, og * 512:(og + 1) * 512], wr=["wb%d" % s])
                    for oc in range(4):
                        ec = og * 4 + oc
                        for t2 in range(n2):
                            a = self.rot("acc", 6)
                            for kc in range(DC):
                                P.pe(lambda e, s=s, kc=kc, oc=oc, t2=t2, a=a: e.matmul(
                                    acc[a][:], wb[s][:, kc, oc * 128:(oc + 1) * 128], src(kc, t2),
                                    start=(kc == 0), stop=(kc == DC - 1)), rd=["wb%d" % s] + skeys, wr=["acc%d" % a])
                            evac(kind, a, dst, ec, tb * TB + t2 * 512)

            def lora(src, skeys, w1, w2, R, act, dst, tb):
                s = self.rot("l1", 2)
                P.dma("pool", l1[s][:, :, 0:R], w1.rearrange("(kc p) r -> p kc r", p=128), wr=["l1%d" % s])
                P.dma("pool", l2[s][0:R, :], w2, wr=["l2%d" % s])
                for t2 in range(n2):
                    a = self.rot("acc", 6)
                    for kc in range(DC):
                        P.pe(lambda e, s=s, kc=kc, t2=t2, a=a: e.matmul(
                            acc[a][0:R, :], l1[s][:, kc, 0:R], src(kc, t2), start=(kc == 0), stop=(kc == DC - 1)),
                            rd=["l1%d" % s] + skeys, wr=["acc%d" % a])
                    q = self.rot("lt", 2)
                    P.act(lambda e, q=q, a=a: e.activation(lt[q][0:R, :], acc[a][0:R, :], act),
                          rd=["acc%d" % a], wr=["lt%d" % q])
                    for ec in range(EC):
                        a2 = self.rot("acc", 6)
                        P.pe(lambda e, s=s, q=q, ec=ec, a2=a2: e.matmul(
                            acc[a2][:], l2[s][0:R, ec * 128:(ec + 1) * 128], lt[q][0:R, :], start=True, stop=True),
                            rd=["l2%d" % s, "lt%d" % q], wr=["acc%d" % a2])
                        evac("f32", a2, dst, ec, tb * TB + t2 * 512)

            import os
            for tb in range(T // TB if os.environ.get("K_RW1") != "skipall" else 0):
                t0 = tb * TB
                lo = max(0, t0 - 64)
                hi = min(T, t0 + TB + 64)
                if lo > t0 - 64:
                    P.pool(lambda e: e.memset(hh[:, :, 0:64], 0.0), wr=["hh"])
                if hi < t0 + TB + 64:
                    P.pool(lambda e: e.memset(hh[:, :, TB + 64:TB + 128], 0.0), wr=["hh"])
                P.dma("sp", hh[:, :, 64 - (t0 - lo):64 + (hi - t0)], hv[:, :, lo:hi],
                      rd=[("hs", tt) for tt in range(self.NT)], wr=["hh"])
                for m in range(7):
                    P.dma("pool", mk[:, m, :], self.masks[5 + m:6 + m, t0:t0 + TB].partition_broadcast(128), wr=["mk7"])
                for c in range(DC):
                    tl = terms[c // 4]
                    q = self.rot("sh", 1)
                    off, m = tl[0]
                    P.dve(lambda e, c=c, off=off, m=m, q=q: e.tensor_tensor(
                        t1s[q][:], hh[:, c, 64 + off:64 + off + TB], mk[:, m, :], ALU.mult), rd=["hh", "mk7"], wr=["sh%d" % q])
                    if len(tl) > 1:
                        off, m = tl[1]
                        P.pool(lambda e, c=c, off=off, m=m, q=q: e.tensor_tensor(
                            t2s[q][:], hh[:, c, 64 + off:64 + off + TB], mk[:, m, :], ALU.mult), rd=["hh", "mk7"], wr=["sh2%d" % q])
                        P.dve(lambda e, q=q: e.tensor_tensor(t1s[q][:], t1s[q][:], t2s[q][:], ALU.add),
                              rd=["sh%d" % q, "sh2%d" % q], wr=["sh%d" % q])
                    P.dve(lambda e, c=c, q=q: e.tensor_tensor(xx[:, c, :], t1s[q][:], hh[:, c, 64:64 + TB], ALU.subtract),
                          rd=["sh%d" % q, "hh"], wr=["xx"])
                import os
                parts = os.environ.get("K_RW1", "r,w,k,v,a,z").split(",")
                xt, xk = mix(0, tb)
                if "r" in parts:
                    proj(lambda kc, t2, xt=xt: xt[:, kc, t2 * 512:(t2 + 1) * 512], xk, "rwkv_w_r", "bf", self.rs, tb)
                xt, xk = mix(1, tb)
                for d in range(2 if "w" in parts else 0):
                    lora(lambda kc, t2, xt=xt: xt[:, kc, t2 * 512:(t2 + 1) * 512], xk,
                         self.w["rwkv_w1"][j, d], self.w["rwkv_w2"][j, d], 128, AF.Tanh, self.wl[d], tb)
                xt, xk = mix(2, tb)
                if "k" in parts:
                    proj(lambda kc, t2, xt=xt: xt[:, kc, t2 * 512:(t2 + 1) * 512], xk, "rwkv_w_k", "f32", self.ks, tb)
                xt, xk = mix(3, tb)
                if "v" in parts:
                    proj(lambda kc, t2, xt=xt: xt[:, kc, t2 * 512:(t2 + 1) * 512], xk, "rwkv_w_v", "f32", self.vs[j], tb)
                if j > 0 and "v" in parts:
                    lora(lambda kc, t2, xt=xt: xt[:, kc, t2 * 512:(t2 + 1) * 512], xk,
                         self.w["rwkv_v1"][j - 1], self.w["rwkv_v2"][j - 1], 96, AF.Copy, self.vmx, tb)
                xt, xk = mix(4, tb)
                for d in range(2 if "a" in parts else 0):
                    lora(lambda kc, t2, xt=xt: xt[:, kc, t2 * 512:(t2 + 1) * 512], xk,
                         self.w["rwkv_a1"][j, d], self.w["rwkv_a2"][j, d], 128, AF.Copy, self.al[d], tb)
                if "z" in parts:
                    proj(lambda kc, t2: hh[:, kc, 64 + t2 * 512:64 + (t2 + 1) * 512], ["hh"], "rwkv_w_g", "silu", self.zs, tb)

    def phase_rw2(self, i, j):
        P = self.P
        T = self.T
        SBK = min(1024, T)
        nsb = T // SBK
        NCH = SBK // 64
        nchunk = T // 64
        CC = -0.6065306597126334
        with self.phase():
            self.load_common()
            cst = self.cst
            f = lambda n: self.sb(n, [128, SBK])
            kt, wlt, alt, vt, vft, vmt = f("kt"), f("wlt"), f("alt"), f("vt"), f("vft"), f("vmt")
            rt = self.sb("rt", [128, SBK], BF16)
            kk, asg, cum, sg, e1, e2, e3, e4, kd, kb, tmp = [f(n) for n in
                ("kk", "asg", "cum", "sg", "e1", "e2", "e3", "e4", "kd", "kb", "tmp")]
            cmk = self.sb("cmk", [128, 2, SBK])
            P.pool(lambda e: e.memset(cmk[:], 1.0), wr=["cmk"])
            P.pool(lambda e: e.memset(cmk[:, 0, 0:SBK:64], 0.0), wr=["cmk"])
            P.pool(lambda e: e.memset(cmk[:, 1, 63:SBK:64], 0.0), wr=["cmk"])
            omka = self.sb("omka", [128, 1])
            AR = [self.sb("AR%d" % s, [128, NCH, 256], BF16) for s in range(1)]
            names = ("Bp", "Kp", "BHp", "KHp", "RKp", "Vp")
            OP = {n: self.sb(n, [128, NCH, 128], BF16) for n in names}
            P.pool(lambda e: e.memset(AR[0][:], 0.0), wr=["ops"])
            for n in names:
                P.pool(lambda e, n=n: e.memset(OP[n][:], 0.0), wr=["ops"])
            wC = self.sb("wC", [128, NCH])
            wCk = self.sb("wCk", [128, NCH])
            ckp = self.sb("ckp", [128, 2, nchunk])
            P.dma("sp", ckp[:], self.ckeep, wr=["ckp"])
            lnw = self.sb("lnw", [128, 64])
            lnb = self.sb("lnb", [128, 64])
            zt = self.sb("zt", [128, T], BF16)
            ygo = self.sb("ygo", [128, T], BF16)
            R3 = 3
            NA = [self.sb("NA%d" % s, [128, 256], BF16) for s in range(R3)]
            AK = [self.sb("AK%d" % s, [128, 256], BF16) for s in range(R3)]
            NT_ = [self.sb("NT%d" % s, [128, 128], BF16) for s in range(R3)]
            PL = [self.sb("PL%d" % s, [128, 256], BF16) for s in range(8)]
            XW = [self.sb("XW%d" % s, [128, 256], BF16) for s in range(6)]
            c2 = self.sb("c2", [128, 1024])
            P.dma("sp", c2[:], self.consts2, wr=["c2"])
            II = self.sb("II", [128, 256], BF16)
            P.act(lambda e: e.copy(II[:, 0:128], cst[:, 0:128]), rd=["cst"], wr=["II"])
            P.act(lambda e: e.copy(II[:, 128:256], cst[:, 0:128]), rd=["cst"], wr=["II"])
            BHT = [self.sb("BHT%d" % s, [128, 128], BF16) for s in range(R3)]
            KHT = [self.sb("KHT%d" % s, [128, 128], BF16) for s in range(R3)]
            VT = [self.sb("VT%d" % s, [128, 128], BF16)[:, 0:64] for s in range(R3)]
            UP = [self.sb("UP%d" % s, [128, 128], BF16)[:, 0:64] for s in range(2)]
            UT = [self.sb("UT%d" % s, [128, 128], BF16)[:, 0:64] for s in range(2)]
            YS = [self.sb("YS%d" % s, [128, 66]) for s in range(3)]
            YF = [self.sb("YF%d" % s, [128, 66]) for s in range(3)]
            S = self.sb("S", [128, 64])
            ST = self.sb("ST", [128, 128], BF16)[:, 0:64]
            So = [self.sb("So%d" % s, [128, 64]) for s in range(2)]
            gst = [self.sb("gst%d" % s, [128, 8]) for s in range(2)]
            gy = [self.sb("gy%d" % s, [128, 64]) for s in range(2)]
            YB = [self.sb("YB%d" % s, [128, 128], BF16) for s in range(2)]
            for s in range(2):
                P.pool(lambda e, s=s: e.memset(YB[s][:], 0.0), wr=["YB%d" % s])
            onec = self.sb("onec", [128, 128], BF16)[:, 0:2]
            P.pool(lambda e: e.memset(onec[:], 1.0), wr=["onec"])
            identb = self.sb("identb", [128, 128], BF16)
            P.act(lambda e: e.copy(identb[:], cst[:, 0:128]), rd=["cst"], wr=["identb"])
            mF = cst[:, 384:640]
            mB = cst[:, 640:896]
            p_ss = self.ps("ss", [128, 512])
            p_tr = [self.ps("tr%d" % s, [128, 512], BF16) for s in range(1)]
            p_a = [self.ps("pa%d" % s, [128, 512]) for s in range(2)]
            p_i = [self.ps("pi%d" % s, [128, 512]) for s in range(2)]
            p_c = [self.ps("pc%d" % s, [128, 512]) for s in range(2)]
            koff = {n: self.pv_off["rwkv_%s%d" % (n, j)] for n in ("k_k", "k_a", "r_k")}
            w0o = self.pv_off["rwkv_w0%d" % j]
            a0o = self.pv_off["rwkv_a0%d" % j]
            v0o = self.pv_off["rwkv_v0"] if j > 0 else None
            fm = lambda ap: ap.rearrange("(c p) t -> c p t", p=128)
            for p in range(EC):
                P.dve(lambda e, p=p: e.tensor_scalar(omka[:], self.pv[:, koff["k_a"] + p:koff["k_a"] + p + 1], -1.0, 1.0,
                                                      ALU.mult, ALU.add), rd=["pv"], wr=["omka"])
                for h in range(2):
                    P.dma("sp", lnw[h * 64:(h + 1) * 64, :], self.w_lnw[j, (2 * p + h) * 64:(2 * p + h + 1) * 64].partition_broadcast(64), wr=["lnw"])
                    P.dma("sp", lnb[h * 64:(h + 1) * 64, :], self.w_lnb[j, (2 * p + h) * 64:(2 * p + h + 1) * 64].partition_broadcast(64), wr=["lnb"])
                P.dma("sp", zt[:], self.zs[p * 128:(p + 1) * 128, :], wr=["zt"])
                for d in range(2):
                    P.dma("sp", S[:], self.rwkv_s0[j, d, p], wr=["S"])
                    P.act(lambda e: e.copy(ST[:], S[:]), rd=["S"], wr=["ST"])
                    order = list(range(nsb)) if d == 0 else list(range(nsb - 1, -1, -1))
                    for sbk in order:
                        t0 = sbk * SBK
                        sl = slice(t0, t0 + SBK)
                        P.dma("sp", kt[:], self.ks[p * 128:(p + 1) * 128, sl], wr=["kt"])
                        P.dma("sp", wlt[:], self.wl[d][p * 128:(p + 1) * 128, sl], wr=["wlt"])
                        P.dma("sp", alt[:], self.al[d][p * 128:(p + 1) * 128, sl], wr=["alt"])
                        P.dma("sp", rt[:], self.rs[p * 128:(p + 1) * 128, sl], wr=["rt"])
                        P.dma("sp", vt[:], self.vs[j][p * 128:(p + 1) * 128, sl], wr=["vt"])
                        if j > 0:
                            P.dma("sp", vft[:], self.vs[0][p * 128:(p + 1) * 128, sl], wr=["vft"])
                            P.dma("sp", vmt[:], self.vmx[p * 128:(p + 1) * 128, sl], wr=["vmt"])
                            P.act(lambda e, p=p: e.activation(vmt[:], vmt[:], AF.Sigmoid, bias=self.pv[:, v0o + p:v0o + p + 1]),
                                  rd=["vmt", "pv"], wr=["vmt"])
                            P.dve(lambda e: e.tensor_tensor(vft[:], vft[:], vt[:], ALU.subtract), rd=["vft", "vt"], wr=["vft"])
                            P.dve(lambda e: e.tensor_tensor(vft[:], vft[:], vmt[:], ALU.mult), rd=["vft", "vmt"], wr=["vft"])
                            P.dve(lambda e: e.tensor_tensor(vt[:], vt[:], vft[:], ALU.add), rd=["vft", "vt"], wr=["vt"])
                        P.dve(lambda e, p=p: e.tensor_scalar(kk[:], kt[:], self.pv[:, koff["k_k"] + p:koff["k_k"] + p + 1], None, ALU.mult),
                              rd=["kt", "pv"], wr=["kk"])
                        for t5 in range(SBK // 512):
                            c5 = slice(t5 * 512, (t5 + 1) * 512)
                            P.act(lambda e, c5=c5: e.activation(tmp[:, c5], kk[:, c5], AF.Square), rd=["kk"], wr=["tmp"])
                            P.pe(lambda e, c5=c5: e.matmul(p_ss[:], cst[:, 256:384], tmp[:, c5], start=True, stop=True),
                                 rd=["tmp", "cst"], wr=["ss"])
                            P.act(lambda e, c5=c5: e.activation(tmp[:, c5], p_ss[:], AF.Sqrt), rd=["ss"], wr=["tmp"])
                        P.dve(lambda e: e.tensor_scalar(tmp[:], tmp[:], 1e-12, None, ALU.max), rd=["tmp"], wr=["tmp"])
                        P.dve(lambda e: e.reciprocal(tmp[:], tmp[:]), rd=["tmp"], wr=["tmp"])
                        P.dve(lambda e: e.tensor_tensor(kk[:], kk[:], tmp[:], ALU.mult), rd=["kk", "tmp"], wr=["kk"])
                        P.act(lambda e, p=p, d=d: e.activation(asg[:], alt[:], AF.Sigmoid,
                                                                bias=self.pv[:, a0o + d * EC + p:a0o + d * EC + p + 1]),
                              rd=["alt", "pv"], wr=["asg"])
                        P.act(lambda e, p=p, d=d: e.activation(sg[:], wlt[:], AF.Sigmoid,
                                                                bias=self.pv[:, w0o + d * EC + p:w0o + d * EC + p + 1]),
                              rd=["wlt", "pv"], wr=["sg"])
                        if d == 0:
                            P.dve(lambda e: e.tensor_tensor_scan(cum[:], cmk[:, 0, :], sg[:], 0.0, ALU.mult, ALU.add),
                                  rd=["cmk", "sg"], wr=["cum"])
                            tot = lambda: cum[:, 63:SBK:64]
                        else:
                            P.dve(lambda e: e.tensor_tensor_scan(cum[:, ::-1], cmk[:, 1, ::-1], sg[:, ::-1], 0.0, ALU.mult, ALU.add),
                                  rd=["cmk", "sg"], wr=["cum"])
                            tot = lambda: cum[:, 0:SBK:64]
                        P.act(lambda e: e.activation(e1[:], cum[:], AF.Exp, scale=CC), rd=["cum"], wr=["e1"])
                        P.act(lambda e: e.activation(e2[:], cum[:], AF.Exp, scale=-CC), rd=["cum"], wr=["e2"])
                        P.dve(lambda e: e.tensor_tensor(tmp[:], cum[:], sg[:], ALU.subtract), rd=["cum", "sg"], wr=["tmp"])
                        P.act(lambda e: e.activation(e3[:], tmp[:], AF.Exp, scale=CC), rd=["tmp"], wr=["e3"])
                        P.dve(lambda e, tot=tot: e.tensor_tensor(
                            tmp[:].rearrange("p (c t) -> p c t", t=64), cum[:].rearrange("p (c t) -> p c t", t=64),
                            tot().unsqueeze(2).to_broadcast([128, NCH, 64]), ALU.subtract), rd=["cum"], wr=["tmp"])
                        P.act(lambda e: e.activation(e4[:], tmp[:], AF.Exp, scale=-CC), rd=["tmp"], wr=["e4"])
                        P.act(lambda e, tot=tot: e.activation(wC[:], tot(), AF.Exp, scale=CC), rd=["cum"], wr=["wC"])
                        c0 = sbk * NCH
                        P.dve(lambda e, d=d, c0=c0: e.tensor_tensor(wCk[:], wC[:], ckp[:, d, c0:c0 + NCH], ALU.mult),
                              rd=["wC", "ckp"], wr=["wCk"])
                        P.dve(lambda e, p=p: e.tensor_scalar(kd[:], asg[:], self.pv[:, koff["k_a"] + p:koff["k_a"] + p + 1], omka[:, 0:1],
                                                              ALU.mult, ALU.add), rd=["asg", "pv", "omka"], wr=["kd"])
                        P.dve(lambda e: e.tensor_tensor(kd[:], kd[:], kt[:], ALU.mult), rd=["kd", "kt"], wr=["kd"])
                        P.pool(lambda e: e.tensor_tensor(kb[:], kk[:], asg[:], ALU.mult), rd=["kk", "asg"], wr=["kb"])
                        k_ = 0
                        for h in range(2):
                            ph = slice(h * 64, (h + 1) * 64)
                            ch = slice(h * 64, (h + 1) * 64)
                            v3 = lambda t, ph=ph: t[ph, :].rearrange("p (c t) -> p c t", t=64)
                            ops = [
                                ("stt", AR[0][ph, :, h * 64:h * 64 + 64], kk, -1.0, e3),
                                ("tt", AR[0][ph, :, 128 + h * 64:128 + h * 64 + 64], rt, e1),
                                ("tt", OP["Bp"][ph, :, ch], kb, e2),
                                ("tt", OP["Kp"][ph, :, ch], kd, e2),
                                ("tt", OP["BHp"][ph, :, ch], kb, e4),
                                ("tt", OP["KHp"][ph, :, ch], kd, e4),
                                ("rk", OP["RKp"][ph, :, ch], rt, kd),
                                ("cp", OP["Vp"][ph, :, ch], vt),
                            ]
                            for o in ops:
                                eng = P.dve if (k_ % 3 == 0 or o[0] in ("stt", "rk")) else P.pool
                                k_ += 1
                                rdk = ["kk", "e1", "e2", "e3", "e4", "kb", "kd", "rt", "vt", "pv"]
                                if o[0] == "stt":
                                    eng(lambda e, o=o, v3=v3: e.scalar_tensor_tensor(o[1], v3(o[2]), o[3], v3(o[4]), ALU.mult, ALU.mult),
                                        rd=rdk, wr=["ops"])
                                elif o[0] == "tt":
                                    eng(lambda e, o=o, v3=v3: e.tensor_tensor(o[1], v3(o[2]), v3(o[3]), ALU.mult), rd=rdk, wr=["ops"])
                                elif o[0] == "rk":
                                    eng(lambda e, o=o, v3=v3, p=p, ph=ph: e.scalar_tensor_tensor(
                                        o[1], v3(o[2]), self.pv[ph, koff["r_k"] + p:koff["r_k"] + p + 1], v3(o[3]), ALU.mult, ALU.mult),
                                        rd=rdk, wr=["ops"])
                                else:
                                    P.act(lambda e, o=o, v3=v3: e.copy(o[1], v3(o[2])), rd=rdk, wr=["ops"])
                        import os
                        LV = int(os.environ.get("K_RW2", "9"))
                        corder = list(range(NCH)) if d == 0 else list(range(NCH - 1, -1, -1))
                        if LV < 1:
                            corder = []
                        msk = mF if d == 0 else mB
                        mskT = (mB if d == 0 else mF)[:, 0:128]
                        for c in corder:
                            gc = sbk * NCH + c
                            u = self.rot("unit", R3)
                            trp = p_tr[0]
                            SUB = os.environ.get("K_SUB", "")
                            P.pe(lambda e, c=c: e.transpose(trp[:, 0:128], OP["BHp"][:, c, :], identb[:]), rd=["ops", "identb"], wr=["tr"])
                            if SUB == "a":
                                continue
                            P.pe(lambda e, c=c: e.transpose(trp[:, 128:256], OP["KHp"][:, c, :], identb[:]), rd=["ops", "identb"], wr=["tr"])
                            P.pe(lambda e, c=c: e.transpose(trp[:, 256:384], OP["Vp"][:, c, :], identb[:]), rd=["ops", "identb"], wr=["tr"])
                            if SUB == "b":
                                continue
                            P.act(lambda e, u=u: e.copy(BHT[u][:], trp[:, 0:128]), rd=["tr"], wr=["BHT%d" % u])
                            if SUB == "c":
                                continue
                            P.act(lambda e, u=u: e.copy(KHT[u][:], trp[:, 128:256]), rd=["tr"], wr=["KHT%d" % u])
                            if SUB == "d":
                                continue
                            P.act(lambda e, u=u: e.copy(VT[u][0:64, :], trp[0:64, 256:320]), rd=["tr"], wr=["VT%d" % u])
                            if SUB == "e":
                                continue
                            P.act(lambda e, u=u: e.copy(VT[u][64:128, :], trp[64:128, 320:384]), rd=["tr"], wr=["VT%d" % u])
                            if LV < 2:
                                continue
                            pa = p_a[self.rot("pa", 2)]
                            pak = "pa%d" % ((self._rot["pa"] - 1) % 2)
                            P.pe(lambda e, c=c, pa=pa: e.matmul(pa[:, 0:256], OP["Bp"][:, c, :], AR[0][:, c, :], start=True, stop=True),
                                 rd=["ops"], wr=[pak])
                            P.pe(lambda e, c=c, pa=pa: e.matmul(pa[:, 256:512], OP["Kp"][:, c, :], AR[0][:, c, :], start=True, stop=True),
                                 rd=["ops"], wr=[pak])
                            P.dve(lambda e, u=u, pa=pa, msk=msk: e.tensor_tensor(NA[u][:], pa[:, 0:256], msk, ALU.mult),
                                  rd=[pak, "cst"], wr=["NA%d" % u])
                            P.dve(lambda e, u=u, pa=pa, msk=msk: e.tensor_tensor(AK[u][:], pa[:, 256:512], msk, ALU.mult),
                                  rd=[pak, "cst"], wr=["AK%d" % u])
                            pi = p_i[self.rot("pi", 2)]
                            pik = "pi%d" % ((self._rot["pi"] - 1) % 2)
                            P.pe(lambda e, c=c, pi=pi: e.matmul(pi[:, 0:128], AR[0][:, c, 0:128], OP["Bp"][:, c, :], start=True, stop=True),
                                 rd=["ops"], wr=[pik])
                            P.act(lambda e, u=u, pi=pi: e.copy(NT_[u][:], pi[:, 0:128]), rd=[pik], wr=["NT%d" % u])
                            P.pool(lambda e, u=u, mskT=mskT: e.tensor_tensor(NT_[u][:], NT_[u][:], mskT, ALU.mult),
                                   rd=["NT%d" % u, "cst"], wr=["NT%d" % u])
                            if LV < 3:
                                continue
                            def nxt_pi():
                                pi = p_i[self.rot("pi", 2)]
                                return pi, "pi%d" % ((self._rot["pi"] - 1) % 2)

                            def nxt(name, lst):
                                q = self.rot(name, len(lst))
                                return lst[q], "%s%d" % (name, q)
                            mo = 0 if d == 0 else 128
                            mt = 128 if d == 0 else 0
                            B8, b8k = nxt("PL", PL)
                            P.pool(lambda e, u=u, B8=B8, mo=mo: e.tensor_tensor(B8[:, 0:128], NA[u][:, 0:128], c2[:, mo:mo + 128], ALU.mult),
                                   rd=["NA%d" % u, "c2"], wr=[b8k])
                            P.pool(lambda e, u=u, B8=B8, mt=mt: e.tensor_tensor(B8[:, 128:256], NT_[u][:], c2[:, mt:mt + 128], ALU.mult),
                                   rd=["NT%d" % u, "c2"], wr=[b8k])
                            XA, xak = nxt("XW", XW)
                            P.pool(lambda e, XA=XA, B8=B8: e.tensor_tensor(XA[:], B8[:], II[:], ALU.add), rd=[b8k, "II"], wr=[xak])
                            pi, pik = nxt_pi()
                            P.pe(lambda e, pi=pi, B8=B8: e.matmul(pi[:, 0:128], B8[:, 128:256], B8[:, 0:128], start=True, stop=True), rd=[b8k], wr=[pik])
                            P.pe(lambda e, pi=pi, B8=B8: e.matmul(pi[:, 128:256], B8[:, 0:128], B8[:, 128:256], start=True, stop=True), rd=[b8k], wr=[pik])
                            P1, p1k = nxt("PL", PL)
                            P.act(lambda e, pi=pi, P1=P1: e.copy(P1[:], pi[:, 0:256]), rd=[pik], wr=[p1k])
                            pi, pik = nxt_pi()
                            P.pe(lambda e, pi=pi, P1=P1, XA=XA: e.matmul(pi[:, 0:128], P1[:, 128:256], XA[:, 0:128], start=True, stop=True), rd=[p1k, xak], wr=[pik])
                            P.pe(lambda e, pi=pi, P1=P1, XA=XA: e.matmul(pi[:, 128:256], XA[:, 0:128], P1[:, 128:256], start=True, stop=True), rd=[p1k, xak], wr=[pik])
                            XB, xbk = nxt("XW", XW)
                            P.dve(lambda e, pi=pi, XA=XA, XB=XB: e.tensor_tensor(XB[:], pi[:, 0:256], XA[:], ALU.add), rd=[pik, xak], wr=[xbk])
                            pi, pik = nxt_pi()
                            P.pe(lambda e, pi=pi, P1=P1: e.matmul(pi[:, 0:128], P1[:, 0:128], P1[:, 128:256], start=True, stop=True), rd=[p1k], wr=[pik])
                            P2, p2k = nxt("PL", PL)
                            P.act(lambda e, pi=pi, P2=P2: e.copy(P2[:, 0:128], pi[:, 0:128]), rd=[pik], wr=[p2k])
                            pi, pik = nxt_pi()
                            P.pe(lambda e, pi=pi, P2=P2, XB=XB: e.matmul(pi[:, 0:128], P2[:, 0:128], XB[:, 0:128], start=True, stop=True), rd=[p2k, xbk], wr=[pik])
                            P.pe(lambda e, pi=pi, P2=P2, XB=XB: e.matmul(pi[:, 128:256], XB[:, 0:128], P2[:, 0:128], start=True, stop=True), rd=[p2k, xbk], wr=[pik])
                            Dc, dck = nxt("XW", XW)
                            P.dve(lambda e, pi=pi, XB=XB, Dc=Dc: e.tensor_tensor(Dc[:], pi[:, 0:256], XB[:], ALU.add), rd=[pik, xbk], wr=[dck])
                            for lvl in range(3):
                                last = (lvl == 2)
                                mo_ = 256 + lvl * 256 + (0 if d == 0 else 128)
                                mt_ = 256 + lvl * 256 + (128 if d == 0 else 0)
                                NO, nok = nxt("PL", PL)
                                P.pool(lambda e, u=u, NO=NO, mt_=mt_: e.tensor_tensor(NO[:, 128:256], NT_[u][:], c2[:, mt_:mt_ + 128], ALU.mult),
                                       rd=["NT%d" % u, "c2"], wr=[nok])
                                if not last:
                                    P.pool(lambda e, u=u, NO=NO, mo_=mo_: e.tensor_tensor(NO[:, 0:128], NA[u][:, 0:128], c2[:, mo_:mo_ + 128], ALU.mult),
                                           rd=["NA%d" % u, "c2"], wr=[nok])
                                pi, pik = nxt_pi()
                                P.pe(lambda e, pi=pi, NO=NO, Dc=Dc: e.matmul(pi[:, 0:128], NO[:, 128:256], Dc[:, 0:128], start=True, stop=True), rd=[nok, dck], wr=[pik])
                                if not last:
                                    P.pe(lambda e, pi=pi, NO=NO, Dc=Dc: e.matmul(pi[:, 128:256], NO[:, 0:128], Dc[:, 128:256], start=True, stop=True), rd=[nok, dck], wr=[pik])
                                WW, wwk = nxt("PL", PL)
                                if not last:
                                    P.act(lambda e, pi=pi, WW=WW: e.copy(WW[:], pi[:, 0:256]), rd=[pik], wr=[wwk])
                                else:
                                    P.act(lambda e, pi=pi, WW=WW: e.copy(WW[:, 0:128], pi[:, 0:128]), rd=[pik], wr=[wwk])
                                pi, pik = nxt_pi()
                                P.pe(lambda e, pi=pi, WW=WW, Dc=Dc: e.matmul(pi[:, 0:128], Dc[:, 128:256], WW[:, 0:128], start=True, stop=True), rd=[wwk, dck], wr=[pik])
                                if not last:
                                    P.pe(lambda e, pi=pi, WW=WW, Dc=Dc: e.matmul(pi[:, 128:256], Dc[:, 0:128], WW[:, 128:256], start=True, stop=True), rd=[wwk, dck], wr=[pik])
                                Dn, dnk = nxt("XW", XW)
                                if not last:
                                    P.dve(lambda e, pi=pi, Dc=Dc, Dn=Dn: e.tensor_tensor(Dn[:], pi[:, 0:256], Dc[:], ALU.add), rd=[pik, dck], wr=[dnk])
                                else:
                                    P.dve(lambda e, pi=pi, Dc=Dc, Dn=Dn: e.tensor_tensor(Dn[:, 0:128], pi[:, 0:128], Dc[:, 0:128], ALU.add), rd=[pik, dck], wr=[dnk])
                                Dc, dck = Dn, dnk
                            Xinv, xk_ = Dc[:, 0:128], dck
                            if LV < 4:
                                continue
                            pc = p_c[self.rot("pc", 2)]
                            pck = "pc%d" % ((self._rot["pc"] - 1) % 2)
                            uq = self.rot("UP", 2)
                            P.pe(lambda e, c=c, pc=pc: e.matmul(pc[:, 0:64], AR[0][:, c, 0:128], ST[:], start=True, stop=False),
                                 rd=["ops", "ST"], wr=[pck])
                            P.pe(lambda e, u=u, pc=pc: e.matmul(pc[:, 0:64], AK[u][:, 0:128], VT[u][:], start=False, stop=True),
                                 rd=["AK%d" % u, "VT%d" % u], wr=[pck])
                            P.act(lambda e, uq=uq, pc=pc: e.copy(UP[uq][:], pc[:, 0:64]), rd=[pck], wr=["UP%d" % uq])
                            P.pe(lambda e, Xinv=Xinv, uq=uq, pc=pc: e.matmul(pc[:, 64:128], Xinv, UP[uq][:], start=True, stop=True),
                                 rd=[xk_, "UP%d" % uq], wr=[pck])
                            P.act(lambda e, uq=uq, pc=pc: e.copy(UT[uq][:], pc[:, 64:128]), rd=[pck], wr=["UT%d" % uq])
                            P.pe(lambda e, c=c, pc=pc: e.matmul(pc[:, 128:192], AR[0][:, c, 128:256], ST[:], start=True, stop=False),
                                 rd=["ops", "ST"], wr=[pck])
                            P.pe(lambda e, u=u, uq=uq, pc=pc: e.matmul(pc[:, 128:192], NA[u][:, 128:256], UT[uq][:], start=False, stop=False),
                                 rd=["NA%d" % u, "UT%d" % uq], wr=[pck])
                            P.pe(lambda e, u=u, pc=pc: e.matmul(pc[:, 128:192], AK[u][:, 128:256], VT[u][:], start=False, stop=True),
                                 rd=["AK%d" % u, "VT%d" % u], wr=[pck])
                            P.pe(lambda e, c=c, pc=pc: e.matmul(pc[:, 192:194], OP["RKp"][:, c, :], onec[:], start=True, stop=True),
                                 rd=["ops", "onec"], wr=[pck])
                            P.pe(lambda e, u=u, uq=uq, pc=pc: e.matmul(pc[:, 256:320], BHT[u][:], UT[uq][:], start=True, stop=False),
                                 rd=["BHT%d" % u, "UT%d" % uq], wr=[pck])
                            P.pe(lambda e, u=u, pc=pc: e.matmul(pc[:, 256:320], KHT[u][:], VT[u][:], start=False, stop=True),
                                 rd=["KHT%d" % u, "VT%d" % u], wr=[pck])
                            P.dve(lambda e, c=c, pc=pc: e.scalar_tensor_tensor(S[:], S[:], wCk[:, c:c + 1], pc[:, 256:320], ALU.mult, ALU.add),
                                  rd=["S", "wCk", pck], wr=["S"])
                            nxt = gc + 1 if d == 0 else gc - 1
                            if 0 <= nxt < nchunk:
                                P.act(lambda e, d=d, nxt=nxt: e.activation(ST[:], S[:], AF.Copy, scale=ckp[:, d, nxt:nxt + 1]),
                                      rd=["S", "ckp"], wr=["ST"])
                            seq_end = (gc % 4 == 3) if d == 0 else (gc % 4 == 0)
                            if seq_end:
                                so = self.rot("So", 2)
                                P.act(lambda e, so=so: e.copy(So[so][:], S[:]), rd=["S"], wr=["So%d" % so])
                                P.dma("sp", self.out_rwkv[gc // 4, j, d, p], So[so][:], rd=["So%d" % so])
                            if LV < 5:
                                continue
                            yq = self.rot("YS", 3)
                            P.dve(lambda e, yq=yq, pc=pc: e.tensor_copy(YS[yq][:, 0:65], pc[:, 128:193]), rd=[pck], wr=["YS%d" % yq])
                            if d == 0:
                                P.dma("sp", self.yf[p, gc], YS[yq][:], rd=["YS%d" % yq], wr=[("yf", gc)])
                            else:
                                P.dma("sp", YF[yq][:], self.yf[p, gc], rd=[("yf", gc)], wr=["YF%d" % yq])
                                P.pool(lambda e, yq=yq: e.tensor_tensor(YS[yq][:, 0:65], YS[yq][:, 0:65], YF[yq][:, 0:65], ALU.add),
                                       rd=["YS%d" % yq, "YF%d" % yq], wr=["YS%d" % yq])
                                g = self.rot("gst", 2)
                                P.dve(lambda e, yq=yq, g=g: e.bn_stats(gst[g][:, 0:6], YS[yq][:, 0:64]), rd=["YS%d" % yq], wr=["gst%d" % g])
                                P.dve(lambda e, g=g: e.bn_aggr(gst[g][:, 6:8], gst[g][:, 0:6]), rd=["gst%d" % g], wr=["gst%d" % g])
                                P.act(lambda e, g=g: e.activation(gst[g][:, 7:8], gst[g][:, 7:8], AF.Sqrt, bias=cst[:, 961:962]),
                                      rd=["gst%d" % g, "cst"], wr=["gst%d" % g])
                                P.dve(lambda e, g=g: e.reciprocal(gst[g][:, 7:8], gst[g][:, 7:8]), rd=["gst%d" % g], wr=["gst%d" % g])
                                P.dve(lambda e, yq=yq, g=g: e.tensor_scalar(gy[g][:], YS[yq][:, 0:64], gst[g][:, 6:7], gst[g][:, 7:8],
                                                                             ALU.subtract, ALU.mult), rd=["YS%d" % yq, "gst%d" % g], wr=["gy%d" % g])
                                P.pool(lambda e, g=g: e.tensor_tensor(gy[g][:], gy[g][:], lnw[:], ALU.mult), rd=["gy%d" % g, "lnw"], wr=["gy%d" % g])
                                P.pool(lambda e, g=g: e.tensor_tensor(gy[g][:], gy[g][:], lnb[:], ALU.add), rd=["gy%d" % g, "lnb"], wr=["gy%d" % g])
                                for h in range(2):
                                    ph = slice(h * 64, (h + 1) * 64)
                                    P.dve(lambda e, yq=yq, g=g, u=u, ph=ph, h=h: e.scalar_tensor_tensor(
                                        YB[g][ph, h * 64:(h + 1) * 64], VT[u][ph, :], YS[yq][ph, 64:65], gy[g][ph, :], ALU.mult, ALU.add),
                                        rd=["VT%d" % u, "YS%d" % yq, "gy%d" % g], wr=["YB%d" % g])
                                P.pe(lambda e, g=g: e.transpose(trp[:, 384:512], YB[g][:], identb[:]), rd=["YB%d" % g, "identb"], wr=["tr"])
                                for h in range(2):
                                    ph = slice(h * 64, (h + 1) * 64)
                                    P.dve(lambda e, ph=ph, h=h, gc=gc: e.tensor_tensor(
                                        ygo[ph, gc * 64:(gc + 1) * 64], trp[ph, 384 + h * 64:384 + (h + 1) * 64],
                                        zt[ph, gc * 64:(gc + 1) * 64], ALU.mult), rd=["tr", "zt"], wr=["ygo"])
                P.dma("sp", self.ygs[p * 128:(p + 1) * 128, :], ygo[:], rd=["ygo"])

    def build(self):
        self.phase_adaln()
        self.phase_t0()
        for i in range(self.depth):
            j = i // 2
            if i % 2 == 0:
                self.phase_lru1(i, j)
                self.phase_lru2(i, j)
                self.phase_outproj(i, "lru_w_out", j)
            else:
                self.rwkv_layer(i, j)
        self.phase_tf()
        return self.nc


WEIGHT_SHAPES = {
    "ada_w": (DEPTH, D, 3 * D),
    "lru_w_in": (2, D, 2 * E),
    "lru_gate_w": (2, 2, 2, 16, 256, 256),
    "lru_w_out": (2, E, D),
    "rwkv_w_r": (2, D, E), "rwkv_w_k": (2, D, E), "rwkv_w_v": (2, D, E), "rwkv_w_g": (2, D, E),
    "rwkv_w_o": (2, E, D),
    "rwkv_w1": (2, 2, D, 128), "rwkv_w2": (2, 2, 128, E),
    "rwkv_a1": (2, 2, D, 128), "rwkv_a2": (2, 2, 128, E),
    "rwkv_v1": (1, D, 96), "rwkv_v2": (1, 96, E),
}


def build_pvec(inp, cond):
    pv = PV()
    pv.put("cond", cond)
    for i in range(DEPTH):
        pv.put("norm_pre%d" % i, inp["norm_pre"][i])
        pv.put("norm_post%d" % i, inp["norm_post"][i])
        pv.put("ada_b%d" % i, inp["ada_b"][i])
    for j in range(2):
        pv.put("lru_conv_w%d" % j, np.asarray(inp["lru_conv_w"][j]).reshape(-1))
        pv.put("lru_conv_b%d" % j, inp["lru_conv_b"][j])
        pv.put("lru_gate_b%d" % j, np.asarray(inp["lru_gate_b"][j]).reshape(-1))
        pv.put("lru_lambda%d" % j, np.asarray(inp["lru_lambda"][j]).reshape(-1))
    for j in range(2):
        pv.put("rwkv_mu%d" % j, np.asarray(inp["rwkv_mu"][j]).reshape(-1))
        pv.put("rwkv_w0%d" % j, np.asarray(inp["rwkv_w0"][j]).reshape(-1))
        pv.put("rwkv_a0%d" % j, np.asarray(inp["rwkv_a0"][j]).reshape(-1))
        for n in ("k_k", "k_a", "r_k"):
            pv.put("rwkv_%s%d" % (n, j), inp["rwkv_" + n][j])
    pv.put("rwkv_v0", inp["rwkv_v0"][0])
    return pv


def core_inputs(inp, kind, idx, T):
    f = lambda a: np.ascontiguousarray(np.asarray(a, np.float32))
    if kind == "sample":
        xin = f(inp["x_sample"][idx])
        cond = f(inp["c"][idx])
        h0 = _pm(f(inp["state_lru"][idx]).reshape(-1))
        masks = make_masks(T, T, True)
    else:
        nseq = T // 256
        if kind == "prompt":
            xin = f(inp["x_prompt"][idx * nseq:(idx + 1) * nseq]).reshape(T, D)
        else:
            xin = np.zeros((T, D), np.float32)
        cond = f(inp["c_ctx"])
        h0 = np.zeros((128, 2 * 2 * EC), np.float32)
        masks = make_masks(T, 256, False)
    pv = build_pvec(inp, cond)
    nchunk = T // 64
    ck = np.ones((128, 2, nchunk), np.float32)
    if kind == "sample":
        sr = f(inp["state_rwkv"][idx]).reshape(2, 2, EC, 2, 64, 64)
        s0 = np.ascontiguousarray(sr.transpose(0, 1, 2, 3, 5, 4)).reshape(2, 2, EC, 128, 64)
    else:
        s0 = np.zeros((2, 2, EC, 128, 64), np.float32)
        c = np.arange(nchunk)
        ck[:, 0, :] = ~((c % 4 == 0) & (c > 0))
        ck[:, 1, :] = ~((c % 4 == 3) & (c < nchunk - 1))
    m = {"xin": xin, "pvec": pv.arr(), "masks": masks, "consts": make_consts(), "consts2": make_consts2(), "lru_h0": h0,
         "rwkv_s0": s0, "ckeep": ck, "rwkv_ln_w": f(inp["rwkv_ln_w"]), "rwkv_ln_b": f(inp["rwkv_ln_b"])}
    for n in WEIGHT_SHAPES:
        m[n] = f(inp[n])
    return m, pv


_CACHE = {}


def kernel(**inp):
    T = 4096
    plan = [("sample", b) for b in range(4)] + [("prompt", 0), ("prompt", 1), ("idle", 0), ("idle", 0)]
    maps = []
    pv = None
    for kind, idx in plan:
        m, pv = core_inputs(inp, kind, idx, T)
        maps.append(m)
    if "nc" not in _CACHE:
        _CACHE["nc"] = Builder(T, pv.off, pv.n).build()
    res = run_bass_kernel_spmd(_CACHE["nc"], maps, core_ids=list(range(8)))
    R = res.results
    y_sample = np.stack([np.asarray(R[b]["yout"], np.float32) for b in range(4)], 0)
    y_prompt = np.concatenate([np.asarray(R[4 + i]["yout"], np.float32).reshape(16, 256, D) for i in range(2)], 0)
    st_lru = np.concatenate([np.asarray(R[4 + i]["out_lru"], np.float32) for i in range(2)], 0)
    sr = np.concatenate([np.asarray(R[4 + i]["out_rwkv"], np.float32) for i in range(2)], 0)
    sr = sr.reshape(32, 2, 2, EC, 2, 64, 64).transpose(0, 1, 2, 3, 4, 6, 5).reshape(32, 2, 2, NH, 64, 64)
    return (y_prompt, y_sample, st_lru, np.ascontiguousarray(sr))
```

```python
import contextlib
import numpy as np
import concourse.bass as bass
import concourse.mybir as mybir
from concourse.bass_utils import run_bass_kernel_spmd

F32 = mybir.dt.float32
BF16 = mybir.dt.bfloat16
AF = mybir.ActivationFunctionType
ALU = mybir.AluOpType
AX = mybir.AxisListType

D = 2048
E = 4096
DC = D // 128
EC = E // 128
DEPTH = 4
NH = 64
NORM_EPS = 1e-6
GN_EPS = 64e-5
LRU_C = 8.0
N_DMA_SLOTS = 8


class Op:
    __slots__ = ("eng", "fn", "rd", "wr", "dma", "deps", "sig", "slot", "slot_val", "idx")

    def __init__(self, eng, fn, rd, wr, dma):
        self.eng = eng
        self.fn = fn
        self.rd = rd
        self.wr = wr
        self.dma = dma
        self.deps = ()
        self.sig = 0
        self.slot = -1
        self.slot_val = 0


class Prog:
    ENGS = ("pe", "act", "dve", "pool", "sp")

    def __init__(self, nc, same_engine_sync=True):
        self.nc = nc
        self.ops = []
        self.same_engine_sync = same_engine_sync

    def add(self, eng, fn, rd=(), wr=(), dma=False):
        ex = getattr(self, "excl", None)
        if ex:
            mv = [k for k in rd if k in ex]
            if mv:
                wr = tuple(wr) + tuple(mv)
        op = Op(eng, fn, tuple(rd), tuple(wr), dma)
        self.ops.append(op)
        return op

    def pe(self, fn, rd=(), wr=()):
        return self.add("pe", fn, rd, wr)

    def act(self, fn, rd=(), wr=()):
        return self.add("act", fn, rd, wr)

    def dve(self, fn, rd=(), wr=()):
        return self.add("dve", fn, rd, wr)

    def pool(self, fn, rd=(), wr=()):
        return self.add("pool", fn, rd, wr)

    def dma(self, q, out, in_, rd=(), wr=(), **kw):
        return self.add(q, lambda e: e.dma_start(out=out, in_=in_, **kw), rd, wr, dma=True)

    def emit(self):
        nc = self.nc
        ops = self.ops
        last_w = {}
        readers = {}
        for i, op in enumerate(ops):
            op.idx = i
            deps = set()
            for k in op.rd:
                w = last_w.get(k)
                if w is not None:
                    deps.add(w)
            for k in op.wr:
                w = last_w.get(k)
                if w is not None:
                    deps.add(w)
                r = readers.get(k)
                if r:
                    deps.update(r[0].values())
                    deps.update(r[1])
            deps.discard(i)
            for k in op.rd:
                r = readers.get(k)
                if r is None:
                    r = readers[k] = ({}, [])
                if op.dma:
                    r[1].append(i)
                else:
                    r[0][op.eng] = i
            for k in op.wr:
                last_w[k] = i
                readers[k] = ({}, [])
            dl = []
            for j in deps:
                oj = ops[j]
                if (not oj.dma) and oj.eng == op.eng:
                    if op.eng == "pe" or not self.same_engine_sync:
                        continue
                dl.append(j)
            op.deps = dl
        needed = set()
        for op in ops:
            needed.update(op.deps)
        if not hasattr(self, "cnt"):
            self.cnt = {e: 0 for e in self.ENGS}
            self.slot_rr = {e: 0 for e in self.ENGS}
            self.slot_tot = {e: [0] * N_DMA_SLOTS for e in self.ENGS}
            self.sem_e = {e: nc.alloc_semaphore(name="s_" + e) for e in ("pe", "act", "dve", "pool")}
            self.sem_d = {e: [nc.alloc_semaphore(name="d_%s%d" % (e, s)) for s in range(N_DMA_SLOTS)]
                          for e in ("sp", "pool", "act")}
        cnt, slot_rr, slot_tot = self.cnt, self.slot_rr, self.slot_tot
        for op in ops:
            if op.dma:
                s = slot_rr[op.eng]
                slot_rr[op.eng] = (s + 1) % N_DMA_SLOTS
                op.slot = s
                slot_tot[op.eng][s] += 16
                op.slot_val = slot_tot[op.eng][s]
            elif op.idx in needed:
                cnt[op.eng] += 1
                op.sig = cnt[op.eng]
        stack = contextlib.ExitStack()
        sem_e = self.sem_e
        sem_d = self.sem_d
        block = stack.enter_context(nc.Block())
        per_eng = {e: [op for op in ops if op.eng == e] for e in self.ENGS}

        def make(ename):
            def body(eng):
                waited = {}
                for op in per_eng[ename]:
                    for j in op.deps:
                        oj = ops[j]
                        if oj.dma:
                            sem = sem_d[oj.eng][oj.slot]
                            val = oj.slot_val
                        else:
                            sem = sem_e[oj.eng]
                            val = oj.sig
                        key = id(sem)
                        if waited.get(key, 0) >= val:
                            continue
                        waited[key] = val
                        eng.wait_ge(sem, val)
                    if op.dma:
                        sem = sem_d[ename][op.slot]
                        prev = op.slot_val - 16
                        if prev > 0 and waited.get(id(sem), 0) < prev:
                            eng.wait_ge(sem, prev)
                            waited[id(sem)] = prev
                        op.fn(eng).then_inc(sem, 16)
                    else:
                        ins = op.fn(eng)
                        if op.sig:
                            ins.then_inc(sem_e[ename], 1)
                if ename in sem_d:
                    for s in range(N_DMA_SLOTS):
                        tot = slot_tot[ename][s]
                        if tot > 0 and waited.get(id(sem_d[ename][s]), 0) < tot:
                            eng.wait_ge(sem_d[ename][s], tot)
            return body

        if per_eng["sp"]:
            block.sync(make("sp"))
        if per_eng["pe"]:
            block.tensor(make("pe"))
        if per_eng["act"]:
            block.scalar(make("act"))
        if per_eng["dve"]:
            block.vector(make("dve"))
        if per_eng["pool"]:
            block.gpsimd(make("pool"))
        stack.close()
        self.ops = []


def _pm(v):
    v = np.asarray(v, np.float32)
    return np.ascontiguousarray(v.reshape(-1, 128).T)


class PV:
    def __init__(self):
        self.cols = []
        self.off = {}
        self.n = 0

    def put(self, name, v):
        m = _pm(v)
        self.off[name] = self.n
        self.cols.append(m)
        self.n += m.shape[1]

    def arr(self):
        return np.ascontiguousarray(np.concatenate(self.cols, axis=1))


def make_masks(T, L, grid):
    t = np.arange(T)
    p = t % L
    m = np.zeros((12, T), np.float32)
    m[0] = p >= 2
    m[1] = p >= 1
    m[2] = p <= L - 2
    m[3] = ~((p == 0) & (t > 0))
    m[4] = ~((p == L - 1) & (t < T - 1))
    if grid:
        col = t % 64
        row = (t % L) // 64
        nrow = L // 64
        m[5] = col > 0
        m[6] = col < 63
        m[7] = 0
        m[8] = row > 0
        m[9] = 0
        m[10] = row < nrow - 1
        m[11] = 0
    else:
        m[5] = p > 0
        m[6] = 0
        m[7] = p > 0
        m[8] = 0
        m[9] = p < L - 1
        m[10] = 0
        m[11] = p < L - 1
    return m


def make_consts2():
    c = np.zeros((128, 1024), np.float32)
    s = np.arange(128)
    S, Tt = s[:, None], s[None, :]
    blk = lambda b: (S // b) == (Tt // b)
    up, lo = S < Tt, S > Tt
    c[:, 0:128] = blk(8) & up
    c[:, 128:256] = blk(8) & lo
    for l, b in enumerate((16, 32, 64)):
        off = blk(b) & ~blk(b // 2)
        c[:, 256 + l * 256:384 + l * 256] = off & up
        c[:, 384 + l * 256:512 + l * 256] = off & lo
    return c


def make_consts():
    c = np.zeros((128, 1024), np.float32)
    c[:, 0:128] = np.eye(128)
    c[:, 128:256] = 1.0
    blk = np.zeros((128, 128), np.float32)
    blk[:64, :64] = 1
    blk[64:, 64:] = 1
    c[:, 256:384] = blk
    s = np.arange(128)
    same = (s[:, None] // 64) == (s[None, :] // 64)
    c[:, 384:512] = same & (s[:, None] < s[None, :])
    c[:, 512:640] = same & (s[:, None] <= s[None, :])
    c[:, 640:768] = same & (s[:, None] > s[None, :])
    c[:, 768:896] = same & (s[:, None] >= s[None, :])
    c[:64, 896:960] = np.eye(64)
    c[64:, 896:960] = np.eye(64)
    c[:, 960] = NORM_EPS
    c[:, 961] = GN_EPS
    return c


class Builder:
    def __init__(self, T, pv_off, pv_n, depth=DEPTH, dbg=False):
        self.T = T
        self.pv_off = pv_off
        self.pv_n = pv_n
        self.depth = depth
        self.NT = T // 512
        self.dbg = dbg
        nc = self.nc = bass.Bass("TRN2", target_bir_lowering=False)
        self.P = Prog(nc)
        di = lambda n, s, dt=F32: nc.dram_tensor(n, list(s), dt, kind="ExternalInput").ap()
        do = lambda n, s, dt=F32: nc.dram_tensor(n, list(s), dt, kind="ExternalOutput").ap()
        dx = lambda n, s, dt=F32: nc.dram_tensor(n, list(s), dt, kind=("ExternalOutput" if dbg else "Internal")).ap()
        self.nseq = T // 256
        self.xin = di("xin", [T, D])
        self.pvec = di("pvec", [128, pv_n])
        self.modd = dx("modd", [128, 4 * 80])
        self.masks = di("masks", [12, T])
        self.consts = di("consts", [128, 1024])
        self.consts2 = di("consts2", [128, 1024])
        self.lru_h0 = di("lru_h0", [128, 2 * 2 * EC])
        self.w = {}
        for n, s in WEIGHT_SHAPES.items():
            self.w[n] = di(n, s)
        self.yout = do("yout", [T, D])
        self.out_lru = do("out_lru", [self.nseq, 2, 2, E])
        self.xT = dx("xT", [D, T])
        self.xs = dx("xs", [E, T])
        self.zs = dx("zs", [E, T], BF16)
        self.ygs = dx("ygs", [E, T], BF16)
        self.hs = dx("hs", [D, T], BF16)
        self.rs = dx("rs", [E, T], BF16)
        self.ks = dx("ks", [E, T])
        self.vs = [dx("vs0", [E, T]), dx("vs1", [E, T])]
        self.vmx = dx("vmx", [E, T])
        self.wl = [dx("wl0", [E, T]), dx("wl1", [E, T])]
        self.al = [dx("al0", [E, T]), dx("al1", [E, T])]
        self.yf = dx("yf", [EC, T // 64, 128, 66])
        self.w_lnw = di("rwkv_ln_w", [2, E])
        self.w_lnb = di("rwkv_ln_b", [2, E])
        self.rwkv_s0 = di("rwkv_s0", [2, 2, EC, 128, 64])
        self.ckeep = di("ckeep", [128, 2, T // 64])
        self.out_rwkv = do("out_rwkv", [self.nseq, 2, 2, EC, 128, 64])
        self.dx = dx
        self.di = di
        self.do = do

    @contextlib.contextmanager
    def phase(self):
        st = contextlib.ExitStack()
        self._st = st
        self._rot = {}
        self.P.excl = None
        self._ph = getattr(self, "_ph", 0) + 1
        try:
            yield st
            self.P.emit()
        finally:
            st.close()

    def sb(self, name, shape, dt=F32):
        return self._st.enter_context(self.nc.sbuf_tensor("p%d_%s" % (self._ph, name), list(shape), dt))

    def ps(self, name, shape, dt=F32):
        return self._st.enter_context(self.nc.psum_tensor("p%d_%s" % (self._ph, name), list(shape), dt))

    def rot(self, name, n):
        v = self._rot.get(name, 0)
        self._rot[name] = v + 1
        return v % n

    def load_common(self, want_masks=()):
        P = self.P
        self.pv = self.sb("pv", [128, self.pv_n])
        P.dma("sp", self.pv[:], self.pvec, wr=["pv"])
        self.cst = self.sb("cst", [128, 1024])
        P.dma("sp", self.cst[:], self.consts, wr=["cst"])
        self.ident = self.cst[:, 0:128]
        self.ones = self.cst[:, 128:256]
        self.mod = self.sb("mod", [128, 4 * 80])
        P.dma("sp", self.mod[:], self.modd, wr=["mod"], rd=["modd"])

    def pcol(self, name, c, n=1):
        o = self.pv_off[name] + c
        return self.pv[:, o:o + n]

    def phase_adaln(self):
        P = self.P
        nc = self.nc
        with self.phase():
            pv = self.sb("pv", [128, self.pv_n])
            P.dma("sp", pv[:], self.pvec, wr=["pv"])
            sc = self.sb("sc", [128, DC], BF16)
            co = self.pv_off["cond"]
            P.act(lambda e: e.activation(sc[:], pv[:, co:co + DC], AF.Silu), rd=["pv"], wr=["sc"])
            wb = [self.sb("wb%d" % s, [128, DC, 512], BF16) for s in range(2)]
            mps = self.ps("mps", [128, 4 * 48])
            mod = self.sb("mod", [128, 4 * 80])
            for i in range(self.depth):
                wv = self.w["ada_w"][i].rearrange("(kc p) o -> p kc o", p=128)
                for og in range(12):
                    s = self.rot("wb", 2)
                    P.dma("pool", wb[s][:], wv[:, :, og * 512:(og + 1) * 512], wr=["wb%d" % s])
                    for oc in range(4):
                        col = i * 48 + og * 4 + oc
                        for kc in range(DC):
                            P.pe(lambda e, s=s, kc=kc, oc=oc, col=col: e.matmul(
                                mps[:, col:col + 1], wb[s][:, kc, oc * 128:(oc + 1) * 128], sc[:, kc:kc + 1],
                                start=(kc == 0), stop=(kc == DC - 1)), rd=["wb%d" % s, "sc"], wr=["mps"])
                ab = self.pv_off["ada_b%d" % i]
                P.dve(lambda e, i=i, ab=ab: e.tensor_tensor(mod[:, i * 80:i * 80 + 48], mps[:, i * 48:(i + 1) * 48],
                                                             pv[:, ab:ab + 48], ALU.add), rd=["mps", "pv"], wr=["mod"])
                npre = self.pv_off["norm_pre%d" % i]
                npost = self.pv_off["norm_post%d" % i]
                P.dve(lambda e, i=i, npre=npre: e.scalar_tensor_tensor(
                    mod[:, i * 80 + 48:i * 80 + 64], mod[:, i * 80 + 16:i * 80 + 32], 1.0, pv[:, npre:npre + 16],
                    ALU.add, ALU.mult), rd=["mod", "pv"], wr=["mod"])
                P.dve(lambda e, i=i, npost=npost: e.tensor_tensor(
                    mod[:, i * 80 + 64:i * 80 + 80], mod[:, i * 80 + 32:i * 80 + 48], pv[:, npost:npost + 16],
                    ALU.mult), rd=["mod", "pv"], wr=["mod"])
            P.dma("sp", self.modd, mod[:], rd=["mod"], wr=["modd"])

    def phase_t0(self):
        P = self.P
        with self.phase():
            self.load_common()
            xt = [self.sb("xt%d" % s, [128, 4, D]) for s in range(1)]
            xo = [self.sb("xo%d" % s, [128, DC, 512]) for s in range(2)]
            tp = [self.ps("tp%d" % s, [128, 512]) for s in range(4)]
            xv = self.xin.rearrange("(tt s p) d -> tt p s d", s=4, p=128)
            xTv = self.xT.rearrange("(c p) t -> p c t", p=128)
            for tt in range(self.NT):
                P.dma("sp", xt[0][:], xv[tt], wr=["xt0"])
                so = self.rot("xo", 2)
                for c in range(DC):
                    ts_ = self.rot("tp", 4)
                    for s in range(4):
                        P.pe(lambda e, c=c, s=s, ts_=ts_: e.transpose(
                            tp[ts_][:, s * 128:(s + 1) * 128], xt[0][:, s, c * 128:(c + 1) * 128], self.ident),
                            rd=["xt0", "cst"], wr=["tp%d" % ts_])
                    ev = P.act if c % 2 == 0 else P.dve
                    if c % 2 == 0:
                        P.act(lambda e, c=c, so=so, ts_=ts_: e.copy(xo[so][:, c, :], tp[ts_][:]),
                              rd=["tp%d" % ts_], wr=[("xo", so, c)])
                    else:
                        P.dve(lambda e, c=c, so=so, ts_=ts_: e.tensor_copy(xo[so][:, c, :], tp[ts_][:]),
                              rd=["tp%d" % ts_], wr=[("xo", so, c)])
                P.dma("sp", xTv[:, :, tt * 512:(tt + 1) * 512], xo[so][:],
                      rd=[("xo", so, c) for c in range(DC)], wr=[("xT", tt)])

    def phase_tf(self):
        P = self.P
        with self.phase():
            self.load_common()
            xi = [self.sb("xi%d" % s, [128, DC, 512]) for s in range(1)]
            yo = [self.sb("yo%d" % s, [128, 4, D]) for s in range(2)]
            tp = [self.ps("tp%d" % s, [128, 512]) for s in range(4)]
            yv = self.yout.rearrange("(tt s p) d -> tt p s d", s=4, p=128)
            xTv = self.xT.rearrange("(c p) t -> p c t", p=128)
            for tt in range(self.NT):
                P.dma("sp", xi[0][:], xTv[:, :, tt * 512:(tt + 1) * 512], rd=[("xT", tt)], wr=["xi0"])
                so = self.rot("yo", 2)
                k = 0
                for s in range(4):
                    for cg in range(4):
                        ts_ = self.rot("tp", 4)
                        for c4 in range(4):
                            c = cg * 4 + c4
                            P.pe(lambda e, c=c, c4=c4, s=s, ts_=ts_: e.transpose(
                                tp[ts_][:, c4 * 128:(c4 + 1) * 128], xi[0][:, c, s * 128:(s + 1) * 128], self.ident),
                                rd=["xi0", "cst"], wr=["tp%d" % ts_])
                        if k % 2 == 0:
                            P.act(lambda e, s=s, cg=cg, so=so, ts_=ts_: e.copy(
                                yo[so][:, s, cg * 512:(cg + 1) * 512], tp[ts_][:]),
                                rd=["tp%d" % ts_], wr=[("yo", so, s, cg)])
                        else:
                            P.dve(lambda e, s=s, cg=cg, so=so, ts_=ts_: e.tensor_copy(
                                yo[so][:, s, cg * 512:(cg + 1) * 512], tp[ts_][:]),
                                rd=["tp%d" % ts_], wr=[("yo", so, s, cg)])
                        k += 1
                P.dma("sp", yv[tt], yo[so][:], rd=[("yo", so, s, cg) for s in range(4) for cg in range(4)])

    def pre_setup(self):
        self.pre_xt = [self.sb("pxt%d" % s, [128, DC, 512]) for s in range(2)]
        self.pre_sq = [self.sb("psq%d" % s, [128, 512]) for s in range(4)]
        self.pre_ss = self.ps("pss", [128, 512])
        self.pre_r = [self.sb("prs%d" % s, [128, 512]) for s in range(2)]

    def pre_tile(self, i, tt, dst, dkey):
        P = self.P
        xTv = self.xT.rearrange("(c p) t -> p c t", p=128)
        s = self.rot("pxt", 2)
        xt = self.pre_xt[s]
        xk = "pxt%d" % s
        P.dma("sp", xt[:], xTv[:, :, tt * 512:(tt + 1) * 512], rd=[("xT", tt)], wr=[xk])
        for c in range(DC):
            q = self.rot("psq", 4)
            P.act(lambda e, c=c, q=q: e.activation(self.pre_sq[q][:], xt[:, c, :], AF.Square),
                  rd=[xk], wr=["psq%d" % q])
            P.pe(lambda e, c=c, q=q: e.matmul(self.pre_ss[:], self.ones, self.pre_sq[q][:],
                                               start=(c == 0), stop=(c == DC - 1)),
                 rd=["psq%d" % q, "cst"], wr=["pss"])
        r = self.rot("prs", 2)
        rs = self.pre_r[r]
        rk = "prs%d" % r
        P.act(lambda e: e.activation(rs[:], self.pre_ss[:], AF.Sqrt, bias=self.eps_ap, scale=1.0 / D),
              rd=["pss", "cst"], wr=[rk])
        P.dve(lambda e: e.reciprocal(rs[:], rs[:]), rd=[rk], wr=[rk])
        mo = i * 80
        for c in range(DC):
            q = self.rot("psq", 4)
            P.dve(lambda e, c=c, q=q: e.scalar_tensor_tensor(
                self.pre_sq[q][:], xt[:, c, :], self.mod[:, mo + 48 + c:mo + 49 + c], rs[:], ALU.mult, ALU.mult),
                rd=[xk, rk, "mod"], wr=["psq%d" % q])
            P.act(lambda e, c=c, q=q: e.activation(dst(c), self.pre_sq[q][:], AF.Identity,
                                                    bias=self.mod[:, mo + c:mo + c + 1], scale=1.0),
                  rd=["psq%d" % q, "mod"], wr=[dkey(c)])

    def post_setup(self):
        self.po_x = [self.sb("pox%d" % s, [128, DC, 512]) for s in range(1)]
        self.po_sq = [self.sb("posq%d" % s, [128, 512]) for s in range(4)]
        self.po_ss = self.ps("poss", [128, 512])
        self.po_r = [self.sb("pors%d" % s, [128, 512]) for s in range(2)]

    def post_tile(self, i, tt, src, skey):
        P = self.P
        xTv = self.xT.rearrange("(c p) t -> p c t", p=128)
        xt = self.po_x[0]
        P.dma("sp", xt[:], xTv[:, :, tt * 512:(tt + 1) * 512], rd=[("xT", tt)], wr=["pox0"])
        for c in range(DC):
            q = self.rot("posq", 4)
            P.act(lambda e, c=c, q=q: e.activation(self.po_sq[q][:], src(c), AF.Square),
                  rd=[skey(c)], wr=["posq%d" % q])
            P.pe(lambda e, c=c, q=q: e.matmul(self.po_ss[:], self.ones, self.po_sq[q][:],
                                               start=(c == 0), stop=(c == DC - 1)),
                 rd=["posq%d" % q, "cst"], wr=["poss"])
        r = self.rot("pors", 2)
        rs = self.po_r[r]
        rk = "pors%d" % r
        P.act(lambda e: e.activation(rs[:], self.po_ss[:], AF.Sqrt, bias=self.eps_ap, scale=1.0 / D),
              rd=["poss", "cst"], wr=[rk])
        P.dve(lambda e: e.reciprocal(rs[:], rs[:]), rd=[rk], wr=[rk])
        mo = i * 80
        for c in range(DC):
            q = self.rot("posq", 4)
            P.dve(lambda e, c=c, q=q: e.scalar_tensor_tensor(
                self.po_sq[q][:], src(c), self.mod[:, mo + 64 + c:mo + 65 + c], rs[:], ALU.mult, ALU.mult),
                rd=[skey(c), rk, "mod"], wr=["posq%d" % q])
            P.dve(lambda e, c=c, q=q: e.tensor_tensor(xt[:, c, :], xt[:, c, :], self.po_sq[q][:], ALU.add),
                  rd=["posq%d" % q, "pox0"], wr=["pox0"])
        P.dma("sp", xTv[:, :, tt * 512:(tt + 1) * 512], xt[:], rd=["pox0"], wr=[("xT", tt)])

    def phase_outproj(self, i, wname, j):
        P = self.P
        T = self.T
        TB = min(1024, T)
        with self.phase():
            self.load_common()
            self.eps_ap = self.cst[:, 960:961]
            self.post_setup()
            yg = self.sb("yg", [128, EC, TB], BF16)
            mT = self.sb("mT", [128, DC, TB])
            wb = [self.sb("wo%d" % s, [128, EC, 128], BF16) for s in range(2)]
            acc = [self.ps("acc%d" % s, [128, 512]) for s in range(4)]
            ygv = self.ygs.rearrange("(c p) t -> p c t", p=128)
            wv = self.w[wname][j].rearrange("(kc p) o -> p kc o", p=128)
            for tb in range(T // TB):
                P.dma("sp", yg[:], ygv[:, :, tb * TB:(tb + 1) * TB], rd=[("ygs", tb)], wr=["yg"])
                for oc in range(DC):
                    s = self.rot("wo", 2)
                    P.dma("pool", wb[s][:], wv[:, :, oc * 128:(oc + 1) * 128], wr=["wo%d" % s])
                    for t2 in range(TB // 512):
                        a = self.rot("acc", 4)
                        for kc in range(EC):
                            P.pe(lambda e, s=s, kc=kc, t2=t2, a=a: e.matmul(
                                acc[a][:], wb[s][:, kc, :], yg[:, kc, t2 * 512:(t2 + 1) * 512],
                                start=(kc == 0), stop=(kc == EC - 1)), rd=["wo%d" % s, "yg"], wr=["acc%d" % a])
                        if (oc + t2) % 2 == 0:
                            P.act(lambda e, oc=oc, t2=t2, a=a: e.copy(mT[:, oc, t2 * 512:(t2 + 1) * 512], acc[a][:]),
                                  rd=["acc%d" % a], wr=[("mT", oc, t2)])
                        else:
                            P.dve(lambda e, oc=oc, t2=t2, a=a: e.tensor_copy(mT[:, oc, t2 * 512:(t2 + 1) * 512], acc[a][:]),
                                  rd=["acc%d" % a], wr=[("mT", oc, t2)])
                for t2 in range(TB // 512):
                    tt = tb * (TB // 512) + t2
                    self.post_tile(i, tt, lambda c, t2=t2: mT[:, c, t2 * 512:(t2 + 1) * 512],
                                   lambda c, t2=t2: ("mT", c, t2))

    def phase_lru1(self, i, j):
        P = self.P
        T = self.T
        TB = min(2048, T)
        n4 = TB // 512
        with self.phase():
            self.load_common()
            self.eps_ap = self.cst[:, 960:961]
            self.pre_setup()
            hT = self.sb("hT", [128, DC, TB], BF16)
            wb = [self.sb("wb%d" % s, [128, DC, 512], BF16) for s in range(2)]
            sx = [self.sb("sx%d" % s, [128, 512]) for s in range(3)]
            sz = [self.sb("sz%d" % s, [128, 512], BF16) for s in range(3)]
            acc = [self.ps("acc%d" % s, [128, 512]) for s in range(4)]
            wv = self.w["lru_w_in"][j].rearrange("(kc p) o -> p kc o", p=128)
            for tb in range(T // TB):
                for t4 in range(n4):
                    self.pre_tile(i, tb * n4 + t4,
                                  lambda c, t4=t4: hT[:, c, t4 * 512:(t4 + 1) * 512],
                                  lambda c, t4=t4: ("hT", t4))
                for og in range(16):
                    s = self.rot("wb", 2)
                    P.dma("pool", wb[s][:], wv[:, :, og * 512:(og + 1) * 512], wr=["wb%d" % s])
                    for oc in range(4):
                        ec = (og % 8) * 4 + oc
                        for t4 in range(n4):
                            a = self.rot("acc", 4)
                            for kc in range(DC):
                                P.pe(lambda e, s=s, kc=kc, oc=oc, t4=t4, a=a: e.matmul(
                                    acc[a][:], wb[s][:, kc, oc * 128:(oc + 1) * 128],
                                    hT[:, kc, t4 * 512:(t4 + 1) * 512],
                                    start=(kc == 0), stop=(kc == DC - 1)),
                                    rd=["wb%d" % s, ("hT", t4)], wr=["acc%d" % a])
                            t0 = tb * TB + t4 * 512
                            if og < 8:
                                q = self.rot("sx", 3)
                                P.dve(lambda e, q=q, a=a: e.tensor_copy(sx[q][:], acc[a][:]),
                                      rd=["acc%d" % a], wr=["sx%d" % q])
                                P.dma("sp", self.xs[ec * 128:(ec + 1) * 128, t0:t0 + 512], sx[q][:],
                                      rd=["sx%d" % q], wr=[("xs", ec)])
                            else:
                                q = self.rot("sz", 3)
                                P.act(lambda e, q=q, a=a: e.activation(sz[q][:], acc[a][:], AF.Silu),
                                      rd=["acc%d" % a], wr=["sz%d" % q])
                                P.dma("sp", self.zs[ec * 128:(ec + 1) * 128, t0:t0 + 512], sz[q][:],
                                      rd=["sz%d" % q], wr=[("zs", ec)])

    def phase_lru2(self, i, j):
        P = self.P
        T = self.T
        SBK = min(1024, T)
        nsb = T // SBK
        nseq = self.nseq
        with self.phase():
            self.load_common()
            mk = self.sb("mk", [128, 5, T], BF16)
            for m in range(5):
                P.dma("pool", mk[:, m, :], self.masks[m:m + 1, :].partition_broadcast(128), wr=[("mk", m)])
            h0 = self.sb("h0", [128, 2 * 2 * EC])
            P.dma("sp", h0[:], self.lru_h0, wr=["h0"])
            fin = self.sb("fin", [128, 2, nseq, EC])
            sp8 = self.sb("sp8", [128, 2, EC])
            sp16 = self.sb("sp16", [128, 2, EC])
            lo = self.pv_off["lru_lambda%d" % j]
            P.act(lambda e: e.activation(sp8[:].rearrange("p a b -> p (a b)"), self.pv[:, lo:lo + 2 * EC], AF.Exp, scale=-1.0),
                  rd=["pv"], wr=["sp8"])
            P.act(lambda e: e.activation(sp8[:].rearrange("p a b -> p (a b)"), sp8[:].rearrange("p a b -> p (a b)"), AF.Ln, bias=1.0),
                  rd=["sp8"], wr=["sp8"])
            P.dve(lambda e: e.tensor_scalar(sp16[:].rearrange("p a b -> p (a b)"), sp8[:].rearrange("p a b -> p (a b)"), -2.0 * LRU_C, None, ALU.mult),
                  rd=["sp8"], wr=["sp16"])
            P.dve(lambda e: e.tensor_scalar(sp8[:].rearrange("p a b -> p (a b)"), sp8[:].rearrange("p a b -> p (a b)"), -LRU_C, None, ALU.mult),
                  rd=["sp8"], wr=["sp8"])
            xr = self.sb("xr", [128, 2, T + 4])
            P.pool(lambda e: e.memset(xr[:, :, 0:2], 0.0), wr=["xr_pad"])
            P.pool(lambda e: e.memset(xr[:, :, T + 2:T + 4], 0.0), wr=["xr_pad"])
            xc = self.sb("xc", [128, 2, T])
            xcb = self.sb("xcb", [128, 2, T], BF16)
            tmp = [self.sb("ctmp%d" % s, [128, SBK]) for s in range(2)]
            A = [self.sb("A%d" % s, [128, SBK]) for s in range(2)]
            Bb = [self.sb("B%d" % s, [128, SBK]) for s in range(2)]
            C = [self.sb("C%d" % s, [128, SBK]) for s in range(2)]
            carry = self.sb("carry", [128, 2])
            zt = [self.sb("zt%d" % s, [128, T], BF16) for s in range(2)]
            ygo = [self.sb("ygo%d" % s, [128, T], BF16) for s in range(2)]
            gw = [self.sb("gw%d" % s, [128, 2, 256], BF16) for s in range(2)]
            acc = [self.ps("acc%d" % s, [128, 512]) for s in range(6)]
            cw = self.pv_off["lru_conv_w%d" % j]
            cb = self.pv_off["lru_conv_b%d" % j]
            gbo = self.pv_off["lru_gate_b%d" % j]
            for nb in range(16):
                for o in range(2):
                    ec = nb * 2 + o
                    P.dma("sp", xr[:, o, 2:T + 2], self.xs[ec * 128:(ec + 1) * 128, :], rd=[("xs", ec)], wr=[("xr", o)])
                for o in range(2):
                    ec = nb * 2 + o
                    for sbk in range(nsb):
                        t0 = sbk * SBK
                        xk = ("xc", o, sbk)
                        P.dve(lambda e, o=o, t0=t0, ec=ec: e.tensor_scalar(
                            xc[:, o, t0:t0 + SBK], xr[:, o, 2 + t0:2 + t0 + SBK],
                            self.pv[:, cw + 2 * EC + ec:cw + 2 * EC + ec + 1], self.pv[:, cb + ec:cb + ec + 1],
                            ALU.mult, ALU.add), rd=[("xr", o), "xr_pad", "pv"], wr=[xk])
                        for (tap, off, m) in ((0, -2, 0), (1, -1, 1), (3, 1, 2)):
                            q = self.rot("ctmp", 2)
                            P.pool(lambda e, o=o, t0=t0, off=off, m=m, q=q: e.tensor_tensor(
                                tmp[q][:], xr[:, o, 2 + t0 + off:2 + t0 + off + SBK], mk[:, m, t0:t0 + SBK], ALU.mult),
                                rd=[("xr", o), "xr_pad", ("mk", m)], wr=["ctmp%d" % q])
                            P.dve(lambda e, o=o, t0=t0, tap=tap, q=q, ec=ec: e.scalar_tensor_tensor(
                                xc[:, o, t0:t0 + SBK], tmp[q][:], self.pv[:, cw + tap * EC + ec:cw + tap * EC + ec + 1],
                                xc[:, o, t0:t0 + SBK], ALU.mult, ALU.add), rd=["ctmp%d" % q, xk, "pv"], wr=[xk])
                        P.act(lambda e, o=o, t0=t0: e.copy(xcb[:, o, t0:t0 + SBK], xc[:, o, t0:t0 + SBK]),
                              rd=[xk], wr=[("xcb", o, sbk)])
                for o in range(2):
                    ec = nb * 2 + o
                    P.dma("sp", zt[o][:], self.zs[ec * 128:(ec + 1) * 128, :], rd=[("zs", ec)], wr=[("zt", o)])
                for d in range(2):
                    gws = []
                    for g in range(2):
                        s = self.rot("gw", 2)
                        P.dma("pool", gw[s][:], self.w["lru_gate_w"][j, d, g, nb].rearrange("(kc p) o -> p kc o", p=128),
                              wr=["gw%d" % s])
                        gws.append(s)
                    order = list(range(nsb)) if d == 0 else list(range(nsb - 1, -1, -1))
                    for o in range(2):
                        ec = nb * 2 + o
                        for si, sbk in enumerate(order):
                            t0 = sbk * SBK
                            qa = self.rot("A", 2)
                            qb = self.rot("B", 2)
                            qc = self.rot("C", 2)
                            for g in range(2):
                                dst = A[qa] if g == 0 else Bb[qb]
                                dk = ("A%d" % qa) if g == 0 else ("B%d" % qb)
                                gb_col = gbo + (d * 2 + g) * EC + ec
                                for t5 in range(SBK // 512):
                                    a = self.rot("acc", 6)
                                    for kc in range(2):
                                        P.pe(lambda e, g=g, kc=kc, o=o, t0=t0, t5=t5, a=a: e.matmul(
                                            acc[a][:], gw[gws[g]][:, kc, o * 128:(o + 1) * 128],
                                            xcb[:, kc, t0 + t5 * 512:t0 + (t5 + 1) * 512],
                                            start=(kc == 0), stop=(kc == 1)),
                                            rd=["gw%d" % gws[g], ("xcb", 0, sbk), ("xcb", 1, sbk)], wr=["acc%d" % a])
                                    P.act(lambda e, dst=dst, t5=t5, a=a, gb_col=gb_col: e.activation(
                                        dst[:, t5 * 512:(t5 + 1) * 512], acc[a][:], AF.Sigmoid,
                                        bias=self.pv[:, gb_col:gb_col + 1], scale=1.0),
                                        rd=["acc%d" % a, "pv"], wr=[dk])
                            ak, bk, ck = "A%d" % qa, "B%d" % qb, "C%d" % qc
                            Aq, Bq, Cq = A[qa], Bb[qb], C[qc]
                            P.act(lambda e, Aq=Aq, Cq=Cq, d=d, ec=ec: e.activation(
                                Cq[:], Aq[:], AF.Exp, scale=sp16[:, d, ec:ec + 1]), rd=[ak, "sp16"], wr=[ck])
                            P.act(lambda e, Aq=Aq, d=d, ec=ec: e.activation(
                                Aq[:], Aq[:], AF.Exp, scale=sp8[:, d, ec:ec + 1]), rd=[ak, "sp8"], wr=[ak])
                            P.act(lambda e, Cq=Cq: e.activation(Cq[:], Cq[:], AF.Sqrt, bias=1.0, scale=-1.0),
                                  rd=[ck], wr=[ck])
                            P.dve(lambda e, Bq=Bq, o=o, t0=t0: e.tensor_tensor(Bq[:], Bq[:], xc[:, o, t0:t0 + SBK], ALU.mult),
                                  rd=[bk, ("xc", o, sbk)], wr=[bk])
                            P.dve(lambda e, Bq=Bq, Cq=Cq: e.tensor_tensor(Bq[:], Bq[:], Cq[:], ALU.mult),
                                  rd=[bk, ck], wr=[bk])
                            P.pool(lambda e, Aq=Aq, d=d, t0=t0: e.tensor_tensor(Aq[:], Aq[:], mk[:, 3 + d, t0:t0 + SBK], ALU.mult),
                                   rd=[ak, ("mk", 3 + d)], wr=[ak])
                            yk = ("xr", o)
                            if d == 0:
                                if si == 0:
                                    init = h0[:, (j * 2 + 0) * EC + ec:(j * 2 + 0) * EC + ec + 1]
                                else:
                                    init = xr[:, o, 2 + t0 - 1:2 + t0]
                                P.dve(lambda e, Aq=Aq, Bq=Bq, o=o, t0=t0, init=init: e.tensor_tensor_scan(
                                    xr[:, o, 2 + t0:2 + t0 + SBK], Aq[:], Bq[:], init, ALU.mult, ALU.add),
                                    rd=[ak, bk, yk, "h0"], wr=[yk])
                            else:
                                if si == 0:
                                    init = h0[:, (j * 2 + 1) * EC + ec:(j * 2 + 1) * EC + ec + 1]
                                else:
                                    init = carry[:, o:o + 1]
                                P.dve(lambda e, Aq=Aq, Bq=Bq, Cq=Cq, init=init: e.tensor_tensor_scan(
                                    Cq[:, ::-1], Aq[:, ::-1], Bq[:, ::-1], init, ALU.mult, ALU.add),
                                    rd=[ak, bk, "h0", ("carry", o)], wr=[ck])
                                P.act(lambda e, Cq=Cq, o=o: e.copy(carry[:, o:o + 1], Cq[:, 0:1]),
                                      rd=[ck], wr=[("carry", o)])
                                ns = SBK // 256
                                s0 = t0 // 256
                                P.act(lambda e, Cq=Cq, s0=s0, ns=ns, ec=ec: e.copy(
                                    fin[:, 1, s0:s0 + ns, ec], Cq[:, 0:SBK:256]), rd=[ck], wr=["fin"])
                                P.dve(lambda e, Cq=Cq, o=o, t0=t0: e.tensor_tensor(
                                    xr[:, o, 2 + t0:2 + t0 + SBK], xr[:, o, 2 + t0:2 + t0 + SBK], Cq[:], ALU.add),
                                    rd=[ck, yk], wr=[yk])
                        if d == 0:
                            P.act(lambda e, o=o, ec=ec: e.copy(fin[:, 0, :, ec], xr[:, o, 2 + 255:2 + T:256]),
                                  rd=[("xr", o)], wr=["fin"])
                for o in range(2):
                    ec = nb * 2 + o
                    q = self.rot("ygo", 2)
                    P.dve(lambda e, o=o, q=q: e.tensor_tensor(ygo[q][:], xr[:, o, 2:T + 2], zt[o][:], ALU.mult),
                          rd=[("xr", o), ("zt", o)], wr=["ygo%d" % q])
                    P.dma("sp", self.ygs[ec * 128:(ec + 1) * 128, :], ygo[q][:], rd=["ygo%d" % q],
                          wr=[("ygs", tb) for tb in range(max(1, T // 1024))])
            fo = self.sb("fo", [128, 128])
            tp = self.ps("ftp", [128, 128])
            rows_per = 128 // EC
            for d in range(2):
                for g in range(max(1, nseq // rows_per)):
                    ns = min(rows_per, nseq)
                    n = ns * EC
                    P.pe(lambda e, d=d, g=g, ns=ns, n=n: e.transpose(
                        tp[0:n, :], fin[:, d, g * rows_per:g * rows_per + ns, :].rearrange("p s c -> p (s c)"), self.ident),
                        rd=["fin", "cst"], wr=["ftp"])
                    P.dve(lambda e, n=n: e.tensor_copy(fo[0:n, :], tp[0:n, :]), rd=["ftp"], wr=["fo"])
                    for sl in range(ns):
                        P.dma("sp", self.out_lru[g * rows_per + sl, j, d, :].rearrange("(c p) -> c p", p=128),
                              fo[sl * EC:(sl + 1) * EC, :], rd=["fo"])

    def rwkv_layer(self, i, j):
        import os
        stop = os.environ.get("K_STOP", "")
        self.phase_rw0(i)
        if stop == "rw0":
            return
        self.phase_rw1(i, j)
        if stop == "rw1":
            return
        self.phase_rw2(i, j)
        if stop == "rw2":
            return
        self.phase_outproj(i, "rwkv_w_o", j)

    def phase_rw0(self, i):
        P = self.P
        with self.phase():
            self.load_common()
            self.eps_ap = self.cst[:, 960:961]
            self.pre_setup()
            ho = [self.sb("ho%d" % s, [128, DC, 512], BF16) for s in range(2)]
            hv = self.hs.rearrange("(c p) t -> p c t", p=128)
            for tt in range(self.NT):
                s = self.rot("ho", 2)
                self.pre_tile(i, tt, lambda c, s=s: ho[s][:, c, :], lambda c, s=s: "ho%d" % s)
                P.dma("sp", hv[:, :, tt * 512:(tt + 1) * 512], ho[s][:], rd=["ho%d" % s], wr=[("hs", tt)])

    def phase_rw1(self, i, j):
        P = self.P
        T = self.T
        TB = min(1024, T)
        n2 = TB // 512
        with self.phase():
            self.load_common()
            hh = self.sb("hh", [128, DC, TB + 128], BF16)
            xx = self.sb("xx", [128, DC, TB], BF16)
            xm = [self.sb("xm%d" % s, [128, DC, TB], BF16) for s in range(1)]
            mk = self.sb("mk7", [128, 7, TB], BF16)
            t1s = [self.sb("sh%d" % s, [128, TB]) for s in range(1)]
            t2s = [self.sb("sh2%d" % s, [128, TB]) for s in range(1)]
            wb = [self.sb("wb%d" % s, [128, DC, 512], BF16) for s in range(2)]
            l1 = [self.sb("l1%d" % s, [128, DC, 128], BF16) for s in range(2)]
            l2 = [self.sb("l2%d" % s, [128, E], BF16) for s in range(2)]
            lt = [self.sb("lt%d" % s, [128, 512], BF16) for s in range(2)]
            sf = [self.sb("sf%d" % s, [128, 512]) for s in range(4)]
            sh = [self.sb("sb%d" % s, [128, 512], BF16) for s in range(3)]
            acc = [self.ps("acc%d" % s, [128, 512]) for s in range(6)]
            hv = self.hs.rearrange("(c p) t -> p c t", p=128)
            muo = self.pv_off["rwkv_mu%d" % j]
            terms = {0: [(-1, 0)], 1: [(1, 1), (-1, 2)], 2: [(-64, 3), (1, 4)], 3: [(64, 5), (1, 6)]}
            evk = [0]

            def evac(kind, a, dst_dram, ec, t0):
                if kind in ("bf", "silu"):
                    q = self.rot("sb", 3)
                    if kind == "silu":
                        P.act(lambda e: e.activation(sh[q][:], acc[a][:], AF.Silu), rd=["acc%d" % a], wr=["sb%d" % q])
                    else:
                        P.act(lambda e: e.copy(sh[q][:], acc[a][:]), rd=["acc%d" % a], wr=["sb%d" % q])
                    P.dma("sp", dst_dram[ec * 128:(ec + 1) * 128, t0:t0 + 512], sh[q][:], rd=["sb%d" % q])
                else:
                    q = self.rot("sf", 4)
                    evk[0] += 1
                    if evk[0] % 3 == 0:
                        P.act(lambda e: e.copy(sf[q][:], acc[a][:]), rd=["acc%d" % a], wr=["sf%d" % q])
                    else:
                        P.dve(lambda e: e.tensor_copy(sf[q][:], acc[a][:]), rd=["acc%d" % a], wr=["sf%d" % q])
                    P.dma("sp", dst_dram[ec * 128:(ec + 1) * 128, t0:t0 + 512], sf[q][:], rd=["sf%d" % q])

            def mix(m, tb):
                s = self.rot("xm", 1)
                for c in range(DC):
                    eng = P.dve
                    eng(lambda e, c=c, s=s: e.scalar_tensor_tensor(
                        xm[s][:, c, :], xx[:, c, :], self.pv[:, muo + m * DC + c:muo + m * DC + c + 1],
                        hh[:, c, 64:64 + TB], ALU.mult, ALU.add), rd=["xx", "hh", "pv"], wr=[("xm", s, c)])
                return xm[s], [("xm", s, c) for c in range(DC)]

            def proj(src, skeys, wname, kind, dst, tb):
                wv = self.w[wname][j].rearrange("(kc p) o -> p kc o", p=128)
                for og in range(8):
                    s = self.rot("wb", 2)
                    P.dma("pool", wb[s][:], wv[:, :, og * 512:(og + 1) * 512], wr=["wb%d" % s])
                    for oc in range(4):
                        ec = og * 4 + oc
                        for t2 in range(n2):
                            a = self.rot("acc", 6)
                            for kc in range(DC):
                                P.pe(lambda e, s=s, kc=kc, oc=oc, t2=t2, a=a: e.matmul(
                                    acc[a][:], wb[s][:, kc, oc * 128:(oc + 1) * 128], src(kc, t2),
                                    start=(kc == 0), stop=(kc == DC - 1)), rd=["wb%d" % s] + skeys, wr=["acc%d" % a])
                            evac(kind, a, dst, ec, tb * TB + t2 * 512)

            def lora(src, skeys, w1, w2, R, act, dst, tb):
                s = self.rot("l1", 2)
                P.dma("pool", l1[s][:, :, 0:R], w1.rearrange("(kc p) r -> p kc r", p=128), wr=["l1%d" % s])
                P.dma("pool", l2[s][0:R, :], w2, wr=["l2%d" % s])
                for t2 in range(n2):
                    a = self.rot("acc", 6)
                    for kc in range(DC):
                        P.pe(lambda e, s=s, kc=kc, t2=t2, a=a: e.matmul(
                            acc[a][0:R, :], l1[s][:, kc, 0:R], src(kc, t2), start=(kc == 0), stop=(kc == DC - 1)),
                            rd=["l1%d" % s] + skeys, wr=["acc%d" % a])
                    q = self.rot("lt", 2)
                    P.act(lambda e, q=q, a=a: e.activation(lt[q][0:R, :], acc[a][0:R, :], act),
                          rd=["acc%d" % a], wr=["lt%d" % q])
                    for ec in range(EC):
                        a2 = self.rot("acc", 6)
                        P.pe(lambda e, s=s, q=q, ec=ec, a2=a2: e.matmul(
                            acc[a2][:], l2[s][0:R, ec * 128:(ec + 1) * 128], lt[q][0:R, :], start=True, stop=True),
                            rd=["l2%d" % s, "lt%d" % q], wr=["acc%d" % a2])
                        evac("f32", a2, dst, ec, tb * TB + t2 * 512)

            import os
            for tb in range(T // TB if os.environ.get("K_RW1") != "skipall" else 0):
                t0 = tb * TB
                lo = max(0, t0 - 64)
                hi = min(T, t0 + TB + 64)
                if lo > t0 - 64:
                    P.pool(lambda e: e.memset(hh[:, :, 0:64], 0.0), wr=["hh"])
                if hi < t0 + TB + 64:
                    P.pool(lambda e: e.memset(hh[:, :, TB + 64:TB + 128], 0.0), wr=["hh"])
                P.dma("sp", hh[:, :, 64 - (t0 - lo):64 + (hi - t0)], hv[:, :, lo:hi],
                      rd=[("hs", tt) for tt in range(self.NT)], wr=["hh"])
                for m in range(7):
                    P.dma("pool", mk[:, m, :], self.masks[5 + m:6 + m, t0:t0 + TB].partition_broadcast(128), wr=["mk7"])
                for c in range(DC):
                    tl = terms[c // 4]
                    q = self.rot("sh", 1)
                    off, m = tl[0]
                    P.dve(lambda e, c=c, off=off, m=m, q=q: e.tensor_tensor(
                        t1s[q][:], hh[:, c, 64 + off:64 + off + TB], mk[:, m, :], ALU.mult), rd=["hh", "mk7"], wr=["sh%d" % q])
                    if len(tl) > 1:
                        off, m = tl[1]
                        P.pool(lambda e, c=c, off=off, m=m, q=q: e.tensor_tensor(
                            t2s[q][:], hh[:, c, 64 + off:64 + off + TB], mk[:, m, :], ALU.mult), rd=["hh", "mk7"], wr=["sh2%d" % q])
                        P.dve(lambda e, q=q: e.tensor_tensor(t1s[q][:], t1s[q][:], t2s[q][:], ALU.add),
                              rd=["sh%d" % q, "sh2%d" % q], wr=["sh%d" % q])
                    P.dve(lambda e, c=c, q=q: e.tensor_tensor(xx[:, c, :], t1s[q][:], hh[:, c, 64:64 + TB], ALU.subtract),
                          rd=["sh%d" % q, "hh"], wr=["xx"])
                import os
                parts = os.environ.get("K_RW1", "r,w,k,v,a,z").split(",")
                xt, xk = mix(0, tb)
                if "r" in parts:
                    proj(lambda kc, t2, xt=xt: xt[:, kc, t2 * 512:(t2 + 1) * 512], xk, "rwkv_w_r", "bf", self.rs, tb)
                xt, xk = mix(1, tb)
                for d in range(2 if "w" in parts else 0):
                    lora(lambda kc, t2, xt=xt: xt[:, kc, t2 * 512:(t2 + 1) * 512], xk,
                         self.w["rwkv_w1"][j, d], self.w["rwkv_w2"][j, d], 128, AF.Tanh, self.wl[d], tb)
                xt, xk = mix(2, tb)
                if "k" in parts:
                    proj(lambda kc, t2, xt=xt: xt[:, kc, t2 * 512:(t2 + 1) * 512], xk, "rwkv_w_k", "f32", self.ks, tb)
                xt, xk = mix(3, tb)
                if "v" in parts:
                    proj(lambda kc, t2, xt=xt: xt[:, kc, t2 * 512:(t2 + 1) * 512], xk, "rwkv_w_v", "f32", self.vs[j], tb)
                if j > 0 and "v" in parts:
                    lora(lambda kc, t2, xt=xt: xt[:, kc, t2 * 512:(t2 + 1) * 512], xk,
                         self.w["rwkv_v1"][j - 1], self.w["rwkv_v2"][j - 1], 96, AF.Copy, self.vmx, tb)
                xt, xk = mix(4, tb)
                for d in range(2 if "a" in parts else 0):
                    lora(lambda kc, t2, xt=xt: xt[:, kc, t2 * 512:(t2 + 1) * 512], xk,
                         self.w["rwkv_a1"][j, d], self.w["rwkv_a2"][j, d], 128, AF.Copy, self.al[d], tb)
                if "z" in parts:
                    proj(lambda kc, t2: hh[:, kc, 64 + t2 * 512:64 + (t2 + 1) * 512], ["hh"], "rwkv_w_g", "silu", self.zs, tb)

    def phase_rw2(self, i, j):
        P = self.P
        T = self.T
        SBK = min(512, T)
        nsb = T // SBK
        NCH = SBK // 64
        nchunk = T // 64
        import os
        G = int(os.environ.get('K_G', '4'))
        CC = -0.6065306597126334
        with self.phase():
            self.load_common()
            cst = self.cst
            f = lambda n: self.sb(n, [128, SBK])
            kt, wlt, alt, vt, vft, vmt = f("kt"), f("wlt"), f("alt"), f("vt"), f("vft"), f("vmt")
            rt = self.sb("rt", [128, SBK], BF16)
            kk, asg, cum, sg, e1, e2, e3, e4, kd, kb, tmp = [f(n) for n in
                ("kk", "asg", "cum", "sg", "e1", "e2", "e3", "e4", "kd", "kb", "tmp")]
            cmk = self.sb("cmk", [128, 2, SBK])
            P.pool(lambda e: e.memset(cmk[:], 1.0), wr=["cmk"])
            P.pool(lambda e: e.memset(cmk[:, 0, 0:SBK:64], 0.0), wr=["cmk"])
            P.pool(lambda e: e.memset(cmk[:, 1, 63:SBK:64], 0.0), wr=["cmk"])
            omka = self.sb("omka", [128, 1])
            names = ("Bp", "Kp", "BHp", "KHp", "RKp", "Vp")
            AR, OP, TA, VT, wC, wCk = [], [], [], [], [], []
            for s in range(2):
                AR.append(self.sb("AR%d" % s, [128, NCH, 256], BF16))
                OP.append({n: self.sb("%s%d" % (n, s), [128, NCH, 128], BF16) for n in names})
                TA.append(self.sb("TA%d" % s, [128, NCH, 384], BF16))
                VT.append(self.sb("VT%d" % s, [128, NCH, 128], BF16))
                wC.append(self.sb("wC%d" % s, [128, NCH]))
                wCk.append(self.sb("wCk%d" % s, [128, NCH]))
                P.pool(lambda e, s=s: e.memset(AR[s][:], 0.0), wr=[("ops", s)])
                for n in names:
                    P.pool(lambda e, n=n, s=s: e.memset(OP[s][n][:], 0.0), wr=[("ops", s)])
            ckp = self.sb("ckp", [128, 2, nchunk])
            P.dma("sp", ckp[:], self.ckeep, wr=["ckp"])
            lnw = self.sb("lnw", [128, 64])
            lnb = self.sb("lnb", [128, 64])
            zt = self.sb("zt", [128, T], BF16)
            ygo = self.sb("ygo", [128, T], BF16)
            c2 = self.sb("c2", [128, 1024])
            P.dma("sp", c2[:], self.consts2, wr=["c2"])
            II = self.sb("II", [128, 256], BF16)
            P.act(lambda e: e.copy(II[:, 0:128], cst[:, 0:128]), rd=["cst"], wr=["II"])
            P.act(lambda e: e.copy(II[:, 128:256], cst[:, 0:128]), rd=["cst"], wr=["II"])
            identb = II[:, 0:128]
            RU = 8
            NA = [self.sb("NA%d" % s, [128, 256], BF16) for s in range(RU)]
            AK = [self.sb("AK%d" % s, [128, 256], BF16) for s in range(RU)]
            GG = [self.sb("GG%d" % s, [128, 256], BF16) for s in range(RU)]
            UV = [self.sb("UV%d" % s, [128, 128], BF16) for s in range(RU)]
            SIT = [{n: self.sb("%s_%d" % (n, s), [128, 256], BF16) for n in
                    ("NTc", "B8", "XA", "P1", "XB", "P2", "Dc", "NO0", "NO1", "NO2", "WW", "D1", "D2", "D3", "XV")}
                   for s in range(G)]
            YS = [self.sb("YS%d" % s, [128, 66]) for s in range(3)]
            YF = [self.sb("YF%d" % s, [128, 66]) for s in range(3)]
            S = self.sb("S", [128, 64])
            ST = self.sb("ST", [128, 128], BF16)[:, 0:64]
            So = [self.sb("So%d" % s, [128, 64]) for s in range(2)]
            gst = [self.sb("gst%d" % s, [128, 8]) for s in range(2)]
            gy = [self.sb("gy%d" % s, [128, 64]) for s in range(2)]
            YB = [self.sb("YB%d" % s, [128, 128], BF16) for s in range(2)]
            I2b = self.sb("I2b", [128, 128], BF16)[:, 0:64]
            P.act(lambda e: e.copy(I2b, cst[:, 896:960]), rd=["cst"], wr=["I2b"])
            for s in range(2):
                P.pool(lambda e, s=s: e.memset(YB[s][:], 0.0), wr=["YB%d" % s])
            onec = self.sb("onec", [128, 128], BF16)[:, 0:2]
            P.pool(lambda e: e.memset(onec, 1.0), wr=["onec"])
            mF = cst[:, 384:640]
            mB = cst[:, 640:896]
            p_A = [self.ps("pA%d" % s, [128, 512]) for s in range(G)]
            p_tr = self.ps("ptr", [128, 512], BF16)
            p_ss = self.ps("ss", [128, 512])
            p_S = self.ps("pS", [128, 512])
            p_Y = self.ps("pY", [128, 512])
            koff = {n: self.pv_off["rwkv_%s%d" % (n, j)] for n in ("k_k", "k_a", "r_k")}
            w0o = self.pv_off["rwkv_w0%d" % j]
            a0o = self.pv_off["rwkv_a0%d" % j]
            v0o = self.pv_off["rwkv_v0"] if j > 0 else None

            P.excl = set([("pA", g_) for g_ in range(G)] + ["ss", "ptr0", "pS", "pY"])

            def lockstep(gens):
                act = list(gens)
                while act:
                    nx = []
                    for g in act:
                        try:
                            next(g)
                            nx.append(g)
                        except StopIteration:
                            pass
                    act = nx

            def prep_gen(p, d, sbk, sl):
                t0 = sbk * SBK
                tsl = slice(t0, t0 + SBK)
                ok = ("ops", sl)
                P.dma("sp", kt[:], self.ks[p * 128:(p + 1) * 128, tsl], wr=["kt"])
                P.dma("sp", wlt[:], self.wl[d][p * 128:(p + 1) * 128, tsl], wr=["wlt"])
                P.dma("sp", alt[:], self.al[d][p * 128:(p + 1) * 128, tsl], wr=["alt"])
                P.dma("sp", rt[:], self.rs[p * 128:(p + 1) * 128, tsl], wr=["rt"])
                P.dma("sp", vt[:], self.vs[j][p * 128:(p + 1) * 128, tsl], wr=["vt"])
                if j > 0:
                    P.dma("sp", vft[:], self.vs[0][p * 128:(p + 1) * 128, tsl], wr=["vft"])
                    P.dma("sp", vmt[:], self.vmx[p * 128:(p + 1) * 128, tsl], wr=["vmt"])
                    P.act(lambda e: e.activation(vmt[:], vmt[:], AF.Sigmoid, bias=self.pv[:, v0o + p:v0o + p + 1]),
                          rd=["vmt", "pv"], wr=["vmt"])
                    P.dve(lambda e: e.tensor_tensor(vft[:], vft[:], vt[:], ALU.subtract), rd=["vft", "vt"], wr=["vft"])
                    P.dve(lambda e: e.tensor_tensor(vft[:], vft[:], vmt[:], ALU.mult), rd=["vft", "vmt"], wr=["vft"])
                    P.dve(lambda e: e.tensor_tensor(vt[:], vt[:], vft[:], ALU.add), rd=["vft", "vt"], wr=["vt"])
                    yield
                P.dve(lambda e: e.tensor_scalar(kk[:], kt[:], self.pv[:, koff["k_k"] + p:koff["k_k"] + p + 1], None, ALU.mult),
                      rd=["kt", "pv"], wr=["kk"])
                P.act(lambda e: e.activation(tmp[:], kk[:], AF.Square), rd=["kk"], wr=["tmp"])
                P.pe(lambda e: e.matmul(p_ss[:, 0:SBK], cst[:, 256:384], tmp[:], start=True, stop=True),
                     rd=["tmp", "cst"], wr=["ss"])
                P.act(lambda e: e.activation(tmp[:], p_ss[:, 0:SBK], AF.Sqrt), rd=["ss"], wr=["tmp"])
                yield
                P.dve(lambda e: e.tensor_scalar(tmp[:], tmp[:], 1e-12, None, ALU.max), rd=["tmp"], wr=["tmp"])
                P.dve(lambda e: e.reciprocal(tmp[:], tmp[:]), rd=["tmp"], wr=["tmp"])
                P.dve(lambda e: e.tensor_tensor(kk[:], kk[:], tmp[:], ALU.mult), rd=["kk", "tmp"], wr=["kk"])
                P.act(lambda e: e.activation(asg[:], alt[:], AF.Sigmoid, bias=self.pv[:, a0o + d * EC + p:a0o + d * EC + p + 1]),
                      rd=["alt", "pv"], wr=["asg"])
                P.act(lambda e: e.activation(sg[:], wlt[:], AF.Sigmoid, bias=self.pv[:, w0o + d * EC + p:w0o + d * EC + p + 1]),
                      rd=["wlt", "pv"], wr=["sg"])
                yield
                if d == 0:
                    P.dve(lambda e: e.tensor_tensor_scan(cum[:], cmk[:, 0, :], sg[:], 0.0, ALU.mult, ALU.add),
                          rd=["cmk", "sg"], wr=["cum"])
                    tot = lambda: cum[:, 63:SBK:64]
                else:
                    P.dve(lambda e: e.tensor_tensor_scan(cum[:, ::-1], cmk[:, 1, ::-1], sg[:, ::-1], 0.0, ALU.mult, ALU.add),
                          rd=["cmk", "sg"], wr=["cum"])
                    tot = lambda: cum[:, 0:SBK:64]
                P.act(lambda e: e.activation(e1[:], cum[:], AF.Exp, scale=CC), rd=["cum"], wr=["e1"])
                P.act(lambda e: e.activation(e2[:], cum[:], AF.Exp, scale=-CC), rd=["cum"], wr=["e2"])
                P.dve(lambda e: e.tensor_tensor(tmp[:], cum[:], sg[:], ALU.subtract), rd=["cum", "sg"], wr=["tmp"])
                P.act(lambda e: e.activation(e3[:], tmp[:], AF.Exp, scale=CC), rd=["tmp"], wr=["e3"])
                yield
                P.dve(lambda e: e.tensor_tensor(
                    tmp[:].rearrange("p (c t) -> p c t", t=64), cum[:].rearrange("p (c t) -> p c t", t=64),
                    tot().unsqueeze(2).to_broadcast([128, NCH, 64]), ALU.subtract), rd=["cum"], wr=["tmp"])
                P.act(lambda e: e.activation(e4[:], tmp[:], AF.Exp, scale=-CC), rd=["tmp"], wr=["e4"])
                P.act(lambda e: e.activation(wC[sl][:], tot(), AF.Exp, scale=CC), rd=["cum"], wr=[("wC", sl)])
                c0 = sbk * NCH
                P.dve(lambda e: e.tensor_tensor(wCk[sl][:], wC[sl][:], ckp[:, d, c0:c0 + NCH], ALU.mult),
                      rd=[("wC", sl), "ckp"], wr=[("wCk", sl)])
                P.dve(lambda e: e.tensor_scalar(kd[:], asg[:], self.pv[:, koff["k_a"] + p:koff["k_a"] + p + 1], omka[:, 0:1],
                                                ALU.mult, ALU.add), rd=["asg", "pv", "omka"], wr=["kd"])
                P.dve(lambda e: e.tensor_tensor(kd[:], kd[:], kt[:], ALU.mult), rd=["kd", "kt"], wr=["kd"])
                P.pool(lambda e: e.tensor_tensor(kb[:], kk[:], asg[:], ALU.mult), rd=["kk", "asg"], wr=["kb"])
                yield
                k_ = 0
                rdk = ["kk", "e1", "e2", "e3", "e4", "kb", "kd", "rt", "vt", "pv"]
                for h in range(2):
                    ph = slice(h * 64, (h + 1) * 64)
                    ch = slice(h * 64, (h + 1) * 64)
                    v3 = lambda t, ph=ph: t[ph, :].rearrange("p (c t) -> p c t", t=64)
                    ops = [
                        ("stt", AR[sl][ph, :, h * 64:h * 64 + 64], kk, -1.0, e3),
                        ("tt", AR[sl][ph, :, 128 + h * 64:128 + h * 64 + 64], rt, e1),
                        ("tt", OP[sl]["Bp"][ph, :, ch], kb, e2),
                        ("tt", OP[sl]["Kp"][ph, :, ch], kd, e2),
                        ("tt", OP[sl]["BHp"][ph, :, ch], kb, e4),
                        ("tt", OP[sl]["KHp"][ph, :, ch], kd, e4),
                        ("rk", OP[sl]["RKp"][ph, :, ch], rt, kd),
                        ("cp", OP[sl]["Vp"][ph, :, ch], vt),
                    ]
                    for o in ops:
                        eng = P.dve if (k_ % 3 == 0 or o[0] in ("stt", "rk")) else P.pool
                        k_ += 1
                        if o[0] == "stt":
                            eng(lambda e, o=o, v3=v3: e.scalar_tensor_tensor(o[1], v3(o[2]), o[3], v3(o[4]), ALU.mult, ALU.mult),
                                rd=rdk, wr=[ok])
                        elif o[0] == "tt":
                            eng(lambda e, o=o, v3=v3: e.tensor_tensor(o[1], v3(o[2]), v3(o[3]), ALU.mult), rd=rdk, wr=[ok])
                        elif o[0] == "rk":
                            eng(lambda e, o=o, v3=v3, ph=ph: e.scalar_tensor_tensor(
                                o[1], v3(o[2]), self.pv[ph, koff["r_k"] + p:koff["r_k"] + p + 1], v3(o[3]), ALU.mult, ALU.mult),
                                rd=rdk, wr=[ok])
                        else:
                            P.act(lambda e, o=o, v3=v3: e.copy(o[1], v3(o[2])), rd=rdk, wr=[ok])
                        if k_ % 3 == 0:
                            yield
                yield
                for c in range(NCH):
                    P.pe(lambda e, c=c: e.matmul(p_ss[:, c * 64:(c + 1) * 64], OP[sl]["Vp"][:, c, :], I2b, start=True, stop=True),
                         rd=[ok, "I2b"], wr=["ss"])
                P.act(lambda e: e.copy(VT[sl][:, :, 0:64], p_ss[:, 0:NCH * 64].rearrange("p (c v) -> p c v", v=64)),
                      rd=["ss"], wr=[("VT", sl)])
                yield
                for c in range(NCH):
                    tr = p_tr[:, 0:512]
                    trk = "ptr0"
                    for q, src in enumerate((OP[sl]["BHp"][:, c, :], OP[sl]["KHp"][:, c, :], AR[sl][:, c, 0:128])):
                        P.pe(lambda e, q=q, src=src, tr=tr: e.transpose(tr[:, q * 128:(q + 1) * 128], src, identb), rd=[ok, "II"], wr=[trk])
                    P.act(lambda e, c=c, tr=tr: e.copy(TA[sl][:, c, :], tr[:, 0:384]), rd=[trk], wr=[("TA", sl, c)])
                    if c % 2 == 1:
                        yield

            def si_gen(d, sl, c, u, g):
                ok = ("ops", sl)
                msk = mF if d == 0 else mB
                pA = p_A[g]
                h0k = h1k = ("pA", g)
                Tm = SIT[g]
                tk = lambda n: ("sit", g, n)
                mo = 0 if d == 0 else 128
                mt = 128 if d == 0 else 0
                P.pe(lambda e: e.matmul(pA[:, 0:256], OP[sl]["Bp"][:, c, :], AR[sl][:, c, :], start=True, stop=True), rd=[ok], wr=[h0k])
                P.pe(lambda e: e.matmul(pA[:, 256:512], OP[sl]["Kp"][:, c, :], AR[sl][:, c, :], start=True, stop=True), rd=[ok], wr=[h1k])
                P.dve(lambda e: e.tensor_tensor(NA[u][:], pA[:, 0:256], msk, ALU.mult), rd=[h0k, "cst"], wr=["NA%d" % u])
                P.dve(lambda e: e.tensor_tensor(AK[u][:], pA[:, 256:512], msk, ALU.mult), rd=[h1k, "cst"], wr=["AK%d" % u])
                yield
                P.pe(lambda e: e.matmul(pA[:, 0:128], AR[sl][:, c, 0:128], OP[sl]["Bp"][:, c, :], start=True, stop=True), rd=[ok], wr=[h0k])
                S2 = os.environ.get("K_S2", "")
                if S2 == "mm":
                    yield
                    return
                P.act(lambda e: e.copy(Tm["NTc"][:, 0:128], pA[:, 0:128]), rd=[h0k], wr=[tk("NTc")])
                if S2 == "act":
                    yield
                    return
                B8 = Tm["B8"]
                P.pool(lambda e: e.tensor_tensor(B8[:, 0:128], NA[u][:, 0:128], c2[:, mo:mo + 128], ALU.mult), rd=["NA%d" % u, "c2"], wr=[tk("B8")])
                P.pool(lambda e: e.tensor_tensor(B8[:, 128:256], Tm["NTc"][:, 0:128], c2[:, mt:mt + 128], ALU.mult), rd=[tk("NTc"), "c2"], wr=[tk("B8")])
                XA = Tm["XA"]
                P.pool(lambda e: e.tensor_tensor(XA[:], B8[:], II[:], ALU.add), rd=[tk("B8"), "II"], wr=[tk("XA")])
                yield
                P.pe(lambda e: e.matmul(pA[:, 256:384], B8[:, 128:256], B8[:, 0:128], start=True, stop=True), rd=[tk("B8")], wr=[h1k])
                P.pe(lambda e: e.matmul(pA[:, 384:512], B8[:, 0:128], B8[:, 128:256], start=True, stop=True), rd=[tk("B8")], wr=[h1k])
                P1 = Tm["P1"]
                P.act(lambda e: e.copy(P1[:], pA[:, 256:512]), rd=[h1k], wr=[tk("P1")])
                for l in range(3):
                    NO = Tm["NO%d" % l]
                    mo_ = 256 + l * 256 + (0 if d == 0 else 128)
                    mt_ = 256 + l * 256 + (128 if d == 0 else 0)
                    P.pool(lambda e, NO=NO, mt_=mt_: e.tensor_tensor(NO[:, 128:256], Tm["NTc"][:, 0:128], c2[:, mt_:mt_ + 128], ALU.mult),
                           rd=[tk("NTc"), "c2"], wr=[tk("NO%d" % l)])
                    if l < 2:
                        P.pool(lambda e, NO=NO, mo_=mo_: e.tensor_tensor(NO[:, 0:128], NA[u][:, 0:128], c2[:, mo_:mo_ + 128], ALU.mult),
                               rd=["NA%d" % u, "c2"], wr=[tk("NO%d" % l)])
                yield
                P.pe(lambda e: e.matmul(pA[:, 0:128], P1[:, 128:256], XA[:, 0:128], start=True, stop=True), rd=[tk("P1"), tk("XA")], wr=[h0k])
                P.pe(lambda e: e.matmul(pA[:, 128:256], XA[:, 0:128], P1[:, 128:256], start=True, stop=True), rd=[tk("P1"), tk("XA")], wr=[h0k])
                P.pe(lambda e: e.matmul(pA[:, 256:384], P1[:, 0:128], P1[:, 128:256], start=True, stop=True), rd=[tk("P1")], wr=[h1k])
                XB = Tm["XB"]
                P.dve(lambda e: e.tensor_tensor(XB[:], pA[:, 0:256], XA[:], ALU.add), rd=[h0k, tk("XA")], wr=[tk("XB")])
                P2 = Tm["P2"]
                P.act(lambda e: e.copy(P2[:, 0:128], pA[:, 256:384]), rd=[h1k], wr=[tk("P2")])
                yield
                P.pe(lambda e: e.matmul(pA[:, 0:128], P2[:, 0:128], XB[:, 0:128], start=True, stop=True), rd=[tk("P2"), tk("XB")], wr=[h0k])
                P.pe(lambda e: e.matmul(pA[:, 128:256], XB[:, 0:128], P2[:, 0:128], start=True, stop=True), rd=[tk("P2"), tk("XB")], wr=[h0k])
                Dc, dck = Tm["Dc"], tk("Dc")
                P.dve(lambda e, Dc=Dc: e.tensor_tensor(Dc[:], pA[:, 0:256], XB[:], ALU.add), rd=[h0k, tk("XB")], wr=[dck])
                yield
                for l in range(3):
                    last = (l == 2)
                    NO, nok = Tm["NO%d" % l], tk("NO%d" % l)
                    P.pe(lambda e, NO=NO, Dc=Dc: e.matmul(pA[:, 256:384], NO[:, 128:256], Dc[:, 0:128], start=True, stop=True), rd=[nok, dck], wr=[h1k])
                    if not last:
                        P.pe(lambda e, NO=NO, Dc=Dc: e.matmul(pA[:, 384:512], NO[:, 0:128], Dc[:, 128:256], start=True, stop=True), rd=[nok, dck], wr=[h1k])
                    WW = Tm["WW"]
                    if not last:
                        P.act(lambda e: e.copy(WW[:], pA[:, 256:512]), rd=[h1k], wr=[tk("WW")])
                    else:
                        P.act(lambda e: e.copy(WW[:, 0:128], pA[:, 256:384]), rd=[h1k], wr=[tk("WW")])
                    yield
                    P.pe(lambda e, Dc=Dc: e.matmul(pA[:, 0:128], Dc[:, 128:256], WW[:, 0:128], start=True, stop=True), rd=[tk("WW"), dck], wr=[h0k])
                    if not last:
                        P.pe(lambda e, Dc=Dc: e.matmul(pA[:, 128:256], Dc[:, 0:128], WW[:, 128:256], start=True, stop=True), rd=[tk("WW"), dck], wr=[h0k])
                    Dn, dnk = Tm["D%d" % (l + 1)], tk("D%d" % (l + 1))
                    if not last:
                        P.dve(lambda e, Dc=Dc, Dn=Dn: e.tensor_tensor(Dn[:], pA[:, 0:256], Dc[:], ALU.add), rd=[h0k, dck], wr=[dnk])
                    else:
                        P.dve(lambda e, Dc=Dc, Dn=Dn: e.tensor_tensor(Dn[:, 0:128], pA[:, 0:128], Dc[:, 0:128], ALU.add), rd=[h0k, dck], wr=[dnk])
                    Dc, dck = Dn, dnk
                    yield
                X = Dc[:, 0:128]
                P.pe(lambda e: e.matmul(pA[:, 256:384], X, TA[sl][:, c, 256:384], start=True, stop=True), rd=[dck, ("TA", sl, c)], wr=[h1k])
                P.pe(lambda e: e.matmul(pA[:, 384:448], AK[u][:, 0:128], VT[sl][:, c, 0:64], start=True, stop=True), rd=["AK%d" % u, ("VT", sl)], wr=[h1k])
                XV = Tm["XV"]
                P.act(lambda e: e.copy(XV[:, 0:192], pA[:, 256:448]), rd=[h1k], wr=[tk("XV")])
                yield
                P.pe(lambda e: e.matmul(pA[:, 0:128], XV[:, 0:128], TA[sl][:, c, 0:128], start=True, stop=True), rd=[tk("XV"), ("TA", sl, c)], wr=[h0k])
                P.pe(lambda e: e.matmul(pA[:, 128:256], XV[:, 0:128], NA[u][:, 128:256], start=True, stop=True), rd=[tk("XV"), "NA%d" % u], wr=[h0k])
                P.pe(lambda e: e.matmul(pA[:, 256:320], X, XV[:, 128:192], start=True, stop=True), rd=[dck, tk("XV")], wr=[h1k])
                P.act(lambda e: e.copy(GG[u][:, 0:128], pA[:, 0:128]), rd=[h0k], wr=["GG%d" % u])
                P.dve(lambda e: e.tensor_tensor(GG[u][:, 128:256], pA[:, 128:256], AR[sl][:, c, 128:256], ALU.add), rd=[h0k, ok], wr=["GG%d" % u])
                P.act(lambda e: e.copy(UV[u][:, 0:64], pA[:, 256:320]), rd=[h1k], wr=["UV%d" % u])
                yield

            def chain_gen(p, d, units):
                for (gc, sl, c, u) in units:
                    ok = ("ops", sl)
                    P.pe(lambda e, u=u, sl=sl, c=c: e.matmul(p_S[:, 0:64], TA[sl][:, c, 0:128], UV[u][:, 0:64], start=True, stop=False),
                         rd=[("TA", sl, c), "UV%d" % u], wr=["pS"])
                    P.pe(lambda e, sl=sl, c=c: e.matmul(p_S[:, 0:64], TA[sl][:, c, 128:256], VT[sl][:, c, 0:64], start=False, stop=False),
                         rd=[("TA", sl, c), ("VT", sl)], wr=["pS"])
                    P.pe(lambda e, u=u: e.matmul(p_Y[:, 0:64], NA[u][:, 128:256], UV[u][:, 0:64], start=True, stop=False),
                         rd=["NA%d" % u, "UV%d" % u], wr=["pY"])
                    P.pe(lambda e, u=u, sl=sl, c=c: e.matmul(p_Y[:, 0:64], AK[u][:, 128:256], VT[sl][:, c, 0:64], start=False, stop=False),
                         rd=["AK%d" % u, ("VT", sl)], wr=["pY"])
                    P.pe(lambda e, u=u: e.matmul(p_S[:, 0:64], GG[u][:, 0:128], ST, start=False, stop=True), rd=["GG%d" % u, "ST"], wr=["pS"])
                    P.pe(lambda e, u=u: e.matmul(p_Y[:, 0:64], GG[u][:, 128:256], ST, start=False, stop=True), rd=["GG%d" % u, "ST"], wr=["pY"])
                    P.pe(lambda e, sl=sl, c=c: e.matmul(p_Y[:, 64:66], OP[sl]["RKp"][:, c, :], onec, start=True, stop=True), rd=[ok, "onec"], wr=["pY"])
                    P.dve(lambda e, sl=sl, c=c: e.scalar_tensor_tensor(S[:], S[:], wCk[sl][:, c:c + 1], p_S[:, 0:64], ALU.mult, ALU.add),
                          rd=["S", ("wCk", sl), "pS"], wr=["S"])
                    nxt = gc + 1 if d == 0 else gc - 1
                    if 0 <= nxt < nchunk:
                        P.act(lambda e, nxt=nxt: e.activation(ST, S[:], AF.Copy, scale=ckp[:, d, nxt:nxt + 1]), rd=["S", "ckp"], wr=["ST"])
                    yield
                    seq_end = (gc % 4 == 3) if d == 0 else (gc % 4 == 0)
                    if seq_end:
                        so = self.rot("So", 2)
                        P.act(lambda e, so=so: e.copy(So[so][:], S[:]), rd=["S"], wr=["So%d" % so])
                        P.dma("sp", self.out_rwkv[gc // 4, j, d, p], So[so][:], rd=["So%d" % so])
                    yq = self.rot("YS", 3)
                    P.dve(lambda e, yq=yq: e.tensor_copy(YS[yq][:, 0:65], p_Y[:, 0:65]), rd=["pY"], wr=["YS%d" % yq])
                    if d == 0:
                        P.dma("sp", self.yf[p, gc], YS[yq][:], rd=["YS%d" % yq], wr=[("yf", gc)])
                        yield
                    else:
                        P.dma("sp", YF[yq][:], self.yf[p, gc], rd=[("yf", gc)], wr=["YF%d" % yq])
                        P.pool(lambda e, yq=yq: e.tensor_tensor(YS[yq][:, 0:65], YS[yq][:, 0:65], YF[yq][:, 0:65], ALU.add),
                               rd=["YS%d" % yq, "YF%d" % yq], wr=["YS%d" % yq])
                        g = self.rot("gst", 2)
                        P.dve(lambda e, yq=yq, g=g: e.bn_stats(gst[g][:, 0:6], YS[yq][:, 0:64]), rd=["YS%d" % yq], wr=["gst%d" % g])
                        P.dve(lambda e, g=g: e.bn_aggr(gst[g][:, 6:8], gst[g][:, 0:6]), rd=["gst%d" % g], wr=["gst%d" % g])
                        yield
                        P.act(lambda e, g=g: e.activation(gst[g][:, 7:8], gst[g][:, 7:8], AF.Sqrt, bias=cst[:, 961:962]),
                              rd=["gst%d" % g, "cst"], wr=["gst%d" % g])
                        P.dve(lambda e, g=g: e.reciprocal(gst[g][:, 7:8], gst[g][:, 7:8]), rd=["gst%d" % g], wr=["gst%d" % g])
                        P.dve(lambda e, yq=yq, g=g: e.tensor_scalar(gy[g][:], YS[yq][:, 0:64], gst[g][:, 6:7], gst[g][:, 7:8],
                                                                     ALU.subtract, ALU.mult), rd=["YS%d" % yq, "gst%d" % g], wr=["gy%d" % g])
                        P.pool(lambda e, g=g: e.tensor_tensor(gy[g][:], gy[g][:], lnw[:], ALU.mult), rd=["gy%d" % g, "lnw"], wr=["gy%d" % g])
                        P.pool(lambda e, g=g: e.tensor_tensor(gy[g][:], gy[g][:], lnb[:], ALU.add), rd=["gy%d" % g, "lnb"], wr=["gy%d" % g])
                        yield
                        for h in range(2):
                            ph = slice(h * 64, (h + 1) * 64)
                            P.dve(lambda e, yq=yq, g=g, sl=sl, c=c, ph=ph, h=h: e.scalar_tensor_tensor(
                                YB[g][ph, h * 64:(h + 1) * 64], VT[sl][ph, c, 0:64], YS[yq][ph, 64:65], gy[g][ph, :], ALU.mult, ALU.add),
                                rd=[("VT", sl), "YS%d" % yq, "gy%d" % g], wr=["YB%d" % g])
                        r_ = 0
                        tr = p_tr[:, r_ * 512:(r_ + 1) * 512]
                        trk = "ptr%d" % r_
                        P.pe(lambda e, g=g, tr=tr: e.transpose(tr[:, 0:128], YB[g][:], identb), rd=["YB%d" % g, "II"], wr=[trk])
                        for h in range(2):
                            ph = slice(h * 64, (h + 1) * 64)
                            P.dve(lambda e, ph=ph, h=h, gc=gc, tr=tr: e.tensor_tensor(
                                ygo[ph, gc * 64:(gc + 1) * 64], tr[ph, h * 64:(h + 1) * 64],
                                zt[ph, gc * 64:(gc + 1) * 64], ALU.mult), rd=[trk, "zt"], wr=["ygo"])
                        yield

            import os
            LV = int(os.environ.get("K_RW2", "9"))
            ucount = [0]
            for p in range(EC):
                P.dve(lambda e, p=p: e.tensor_scalar(omka[:], self.pv[:, koff["k_a"] + p:koff["k_a"] + p + 1], -1.0, 1.0,
                                                      ALU.mult, ALU.add), rd=["pv"], wr=["omka"])
                for h in range(2):
                    P.dma("sp", lnw[h * 64:(h + 1) * 64, :], self.w_lnw[j, (2 * p + h) * 64:(2 * p + h + 1) * 64].partition_broadcast(64), wr=["lnw"])
                    P.dma("sp", lnb[h * 64:(h + 1) * 64, :], self.w_lnb[j, (2 * p + h) * 64:(2 * p + h + 1) * 64].partition_broadcast(64), wr=["lnb"])
                P.dma("sp", zt[:], self.zs[p * 128:(p + 1) * 128, :], wr=["zt"])
                for d in range(2):
                    P.dma("sp", S[:], self.rwkv_s0[j, d, p], wr=["S"])
                    P.act(lambda e: e.copy(ST, S[:]), rd=["S"], wr=["ST"])
                    sborder = list(range(nsb)) if d == 0 else list(range(nsb - 1, -1, -1))
                    lockstep([prep_gen(p, d, sborder[0], 0)])
                    pending = []
                    for si, sbk in enumerate(sborder):
                        sl = si % 2
                        corder = list(range(NCH)) if d == 0 else list(range(NCH - 1, -1, -1))
                        nxt_prep = prep_gen(p, d, sborder[si + 1], 1 - sl) if si + 1 < len(sborder) else None
                        for g0 in range(0, NCH, G):
                            grp = []
                            gens = []
                            for gi, c in enumerate(corder[g0:g0 + G]):
                                u = ucount[0] % RU
                                ucount[0] += 1
                                grp.append((sbk * NCH + c, sl, c, u))
                                if LV >= 1:
                                    lim = int(os.environ.get("K_SI", "99"))

                                    def si_lim(gen, lim=lim):
                                        for k_, _ in enumerate(gen):
                                            if k_ + 1 >= lim:
                                                break
                                            yield
                                    gens.append(si_lim(si_gen(d, sl, c, u, gi)))
                            if pending and LV >= 2:
                                gens.append(chain_gen(p, d, pending))
                            if nxt_prep is not None and g0 + G >= NCH:
                                gens.append(nxt_prep)
                                nxt_prep = None
                            lockstep(gens)
                            pending = grp
                        if nxt_prep is not None:
                            lockstep([nxt_prep])
                    if LV >= 2:
                        lockstep([chain_gen(p, d, pending)])
                P.dma("sp", self.ygs[p * 128:(p + 1) * 128, :], ygo[:], rd=["ygo"])

    def build(self):
        self.phase_adaln()
        self.phase_t0()
        for i in range(self.depth):
            j = i // 2
            if i % 2 == 0:
                self.phase_lru1(i, j)
                self.phase_lru2(i, j)
                self.phase_outproj(i, "lru_w_out", j)
            else:
                self.rwkv_layer(i, j)
        self.phase_tf()
        return self.nc


WEIGHT_SHAPES = {
    "ada_w": (DEPTH, D, 3 * D),
    "lru_w_in": (2, D, 2 * E),
    "lru_gate_w": (2, 2, 2, 16, 256, 256),
    "lru_w_out": (2, E, D),
    "rwkv_w_r": (2, D, E), "rwkv_w_k": (2, D, E), "rwkv_w_v": (2, D, E), "rwkv_w_g": (2, D, E),
    "rwkv_w_o": (2, E, D),
    "rwkv_w1": (2, 2, D, 128), "rwkv_w2": (2, 2, 128, E),
    "rwkv_a1": (2, 2, D, 128), "rwkv_a2": (2, 2, 128, E),
    "rwkv_v1": (1, D, 96), "rwkv_v2": (1, 96, E),
}


def build_pvec(inp, cond):
    pv = PV()
    pv.put("cond", cond)
    for i in range(DEPTH):
        pv.put("norm_pre%d" % i, inp["norm_pre"][i])
        pv.put("norm_post%d" % i, inp["norm_post"][i])
        pv.put("ada_b%d" % i, inp["ada_b"][i])
    for j in range(2):
        pv.put("lru_conv_w%d" % j, np.asarray(inp["lru_conv_w"][j]).reshape(-1))
        pv.put("lru_conv_b%d" % j, inp["lru_conv_b"][j])
        pv.put("lru_gate_b%d" % j, np.asarray(inp["lru_gate_b"][j]).reshape(-1))
        pv.put("lru_lambda%d" % j, np.asarray(inp["lru_lambda"][j]).reshape(-1))
    for j in range(2):
        pv.put("rwkv_mu%d" % j, np.asarray(inp["rwkv_mu"][j]).reshape(-1))
        pv.put("rwkv_w0%d" % j, np.asarray(inp["rwkv_w0"][j]).reshape(-1))
        pv.put("rwkv_a0%d" % j, np.asarray(inp["rwkv_a0"][j]).reshape(-1))
        for n in ("k_k", "k_a", "r_k"):
            pv.put("rwkv_%s%d" % (n, j), inp["rwkv_" + n][j])
    pv.put("rwkv_v0", inp["rwkv_v0"][0])
    return pv


def core_inputs(inp, kind, idx, T):
    f = lambda a: np.ascontiguousarray(np.asarray(a, np.float32))
    if kind == "sample":
        xin = f(inp["x_sample"][idx])
        cond = f(inp["c"][idx])
        h0 = _pm(f(inp["state_lru"][idx]).reshape(-1))
        masks = make_masks(T, T, True)
    else:
        nseq = T // 256
        if kind == "prompt":
            xin = f(inp["x_prompt"][idx * nseq:(idx + 1) * nseq]).reshape(T, D)
        else:
            xin = np.zeros((T, D), np.float32)
        cond = f(inp["c_ctx"])
        h0 = np.zeros((128, 2 * 2 * EC), np.float32)
        masks = make_masks(T, 256, False)
    pv = build_pvec(inp, cond)
    nchunk = T // 64
    ck = np.ones((128, 2, nchunk), np.float32)
    if kind == "sample":
        sr = f(inp["state_rwkv"][idx]).reshape(2, 2, EC, 2, 64, 64)
        s0 = np.ascontiguousarray(sr.transpose(0, 1, 2, 3, 5, 4)).reshape(2, 2, EC, 128, 64)
    else:
        s0 = np.zeros((2, 2, EC, 128, 64), np.float32)
        c = np.arange(nchunk)
        ck[:, 0, :] = ~((c % 4 == 0) & (c > 0))
        ck[:, 1, :] = ~((c % 4 == 3) & (c < nchunk - 1))
    m = {"xin": xin, "pvec": pv.arr(), "masks": masks, "consts": make_consts(), "consts2": make_consts2(), "lru_h0": h0,
         "rwkv_s0": s0, "ckeep": ck, "rwkv_ln_w": f(inp["rwkv_ln_w"]), "rwkv_ln_b": f(inp["rwkv_ln_b"])}
    for n in WEIGHT_SHAPES:
        m[n] = f(inp[n])
    return m, pv


_CACHE = {}


def kernel(**inp):
    T = 4096
    plan = [("sample", b) for b in range(4)] + [("prompt", 0), ("prompt", 1), ("idle", 0), ("idle", 0)]
    maps = []
    pv = None
    for kind, idx in plan:
        m, pv = core_inputs(inp, kind, idx, T)
        maps.append(m)
    if "nc" not in _CACHE:
        _CACHE["nc"] = Builder(T, pv.off, pv.n).build()
    res = run_bass_kernel_spmd(_CACHE["nc"], maps, core_ids=list(range(8)))
    R = res.results
    y_sample = np.stack([np.asarray(R[b]["yout"], np.float32) for b in range(4)], 0)
    y_prompt = np.concatenate([np.asarray(R[4 + i]["yout"], np.float32).reshape(16, 256, D) for i in range(2)], 0)
    st_lru = np.concatenate([np.asarray(R[4 + i]["out_lru"], np.float32) for i in range(2)], 0)
    sr = np.concatenate([np.asarray(R[4 + i]["out_rwkv"], np.float32) for i in range(2)], 0)
    sr = sr.reshape(32, 2, 2, EC, 2, 64, 64).transpose(0, 1, 2, 3, 4, 6, 5).reshape(32, 2, 2, NH, 64, 64)
    return (y_prompt, y_sample, st_lru, np.ascontiguousarray(sr))
```
